# Optimizing a Trainium2 kernel written in Bass

```python
import jax, jax.numpy as jnp
from jax import lax
import numpy as np

D_MODEL = 2048
BATCH = 16
SEQ = 256
DEPTH = 1
DEC_BATCH = 4
DEC_SEQ = 1024
PAST_LEN = 512

GRID_W = 64
N_HEADS_A = 8
HEAD_DIM = 128
D_ATTN = N_HEADS_A * HEAD_DIM
KH_MAX = 8
KW = 16
CHUNK = 128
N_GROUPS_B = 8
D_GMLP = 1024
GROUP_CH = D_GMLP // N_GROUPS_B
D_FF = ((8 * D_MODEL // 3 + 255) // 256) * 256
N_MOD = 6
EPS = 1e-6
ATTN_SCALE = HEAD_DIM ** -0.5
SPLITS = (D_ATTN, 2 * D_ATTN, 3 * D_ATTN, 3 * D_ATTN + D_GMLP,
          3 * D_ATTN + 2 * D_GMLP, 3 * D_ATTN + 2 * D_GMLP + D_MODEL)
D_IN = 3 * D_ATTN + 2 * D_GMLP + 2 * D_MODEL

kernel_name = "hybrid_natten_gmlp_prefix_diffusion_step"


def rms_norm(x, g):
    xf = x.astype(jnp.float32)
    y = xf * lax.rsqrt(jnp.mean(xf * xf, axis=-1, keepdims=True) + EPS)
    return (y * g.astype(jnp.float32)).astype(x.dtype)


def layer_norm(x, g):
    xf = x.astype(jnp.float32)
    mu = jnp.mean(xf, axis=-1, keepdims=True)
    xc = xf - mu
    y = xc * lax.rsqrt(jnp.mean(xc * xc, axis=-1, keepdims=True) + EPS)
    return (y * g.astype(jnp.float32)).astype(x.dtype)


def modulation(cvec, w_ada, b_ada):
    m = (jax.nn.silu(cvec) @ w_ada + b_ada)[..., None, :]
    return jnp.split(m, N_MOD, axis=-1)


def window_start(i, k, n):
    return jnp.clip(i - k // 2, 0, n - k)


def context_attention(q, k, v):
    B, L, H, Dh = q.shape
    qb = q.reshape(B, L // CHUNK, CHUNK, H, Dh).transpose(1, 0, 2, 3, 4)

    def block(qi):
        s = jnp.einsum('bqhd,bkhd->bhqk', qi, k).astype(jnp.float32) * ATTN_SCALE
        p = jax.nn.softmax(s, axis=-1).astype(v.dtype)
        return jnp.einsum('bhqk,bkhd->bqhd', p, v)

    o = lax.map(block, qb)
    return o.transpose(1, 0, 2, 3, 4).reshape(B, L, H * Dh)


def neighbourhood_attention(q, k, v, k_ctx, v_ctx, rpb):
    B, N, H, Dh = q.shape
    rows = N // GRID_W
    kh = min(KH_MAX, rows)
    qg = q.reshape(B, rows, GRID_W, H, Dh)
    kg = k.reshape(B, rows, GRID_W, H, Dh)
    vg = v.reshape(B, rows, GRID_W, H, Dh)
    cols = jnp.arange(GRID_W)
    cs = window_start(cols, KW, GRID_W)
    col_mask = (cols[None, :] >= cs[:, None]) & (cols[None, :] < cs[:, None] + KW)
    dc_idx = jnp.clip(cols[None, :] - cols[:, None] + (KW - 1), 0, 2 * KW - 2)
    rpb_cols = rpb[:, :, dc_idx]

    def row_block(r):
        rs = window_start(r, kh, rows)
        q_r = lax.dynamic_index_in_dim(qg, r, axis=1, keepdims=False)
        k_w = lax.dynamic_slice_in_dim(kg, rs, kh, axis=1)
        v_w = lax.dynamic_slice_in_dim(vg, rs, kh, axis=1)
        dr_idx = rs + jnp.arange(kh) - r + (KH_MAX - 1)
        bias = jnp.take(rpb_cols, dr_idx, axis=1).transpose(0, 2, 1, 3)
        s_w = jnp.einsum('bqhd,bjkhd->bhqjk', q_r, k_w).astype(jnp.float32) * ATTN_SCALE
        s_w = s_w + bias[None].astype(jnp.float32)
        s_w = jnp.where(col_mask[None, None, :, None, :], s_w, -jnp.inf)
        s_w = s_w.reshape(B, H, GRID_W, kh * GRID_W)
        s_c = jnp.einsum('bqhd,bchd->bhqc', q_r, k_ctx).astype(jnp.float32) * ATTN_SCALE
        p = jax.nn.softmax(jnp.concatenate([s_w, s_c], axis=-1), axis=-1).astype(v.dtype)
        p_w = p[..., :kh * GRID_W].reshape(B, H, GRID_W, kh, GRID_W)
        p_c = p[..., kh * GRID_W:]
        return (jnp.einsum('bhqjk,bjkhd->bqhd', p_w, v_w)
                + jnp.einsum('bhqc,bchd->bqhd', p_c, v_ctx))

    o = lax.map(row_block, jnp.arange(rows))
    return o.transpose(1, 0, 2, 3, 4).reshape(B, N, H * Dh)


def spatial_gating(u, vb, ln_g, w_s, b_s):
    B, N, _ = u.shape
    vn = layer_norm(vb, ln_g).reshape(B, N // CHUNK, CHUNK, N_GROUPS_B, GROUP_CH)
    s = jnp.einsum('gpq,bnqgc->bnpgc', w_s, vn) + b_s.T[None, None, :, :, None]
    return u * s.reshape(B, N, D_GMLP)


def in_projection(h, P):
    B, N, _ = h.shape
    q, k, v, u, vb, ga, gb = jnp.split(h @ P['w_in'], SPLITS, axis=-1)
    shp = (B, N, N_HEADS_A, HEAD_DIM)
    return q.reshape(shp), k.reshape(shp), v.reshape(shp), u, vb, ga, gb


def merge_branches(o_a, u, vb, ga, gb, P):
    o_b = spatial_gating(jax.nn.gelu(u), jax.nn.gelu(vb), P['ln_v'], P['w_s'], P['b_s'])
    m = jax.nn.sigmoid(ga) * (o_a @ P['w_pa']) + jax.nn.sigmoid(gb) * (o_b @ P['w_pb'])
    return m @ P['w_o']


def ffn_residual(x, shift, scale, gate, P):
    h = rms_norm(x, P['n_ffn_pre']) * (1 + scale) + shift
    y = (jax.nn.silu(h @ P['w_gate']) * (h @ P['w_up'])) @ P['w_down']
    return x + gate * rms_norm(y, P['n_ffn_post'])


def context_layer(x, mods, P):
    sh1, sc1, g1, sh2, sc2, g2 = mods
    h = rms_norm(x, P['n_mix_pre']) * (1 + sc1) + sh1
    q, k, v, u, vb, ga, gb = in_projection(h, P)
    y = merge_branches(context_attention(q, k, v), u, vb, ga, gb, P)
    x = x + g1 * rms_norm(y, P['n_mix_post'])
    return ffn_residual(x, sh2, sc2, g2, P), k, v


def latent_layer(x, mods, k_ctx, v_ctx, P):
    sh1, sc1, g1, sh2, sc2, g2 = mods
    h = rms_norm(x, P['n_mix_pre']) * (1 + sc1) + sh1
    q, k, v, u, vb, ga, gb = in_projection(h, P)
    o_a = neighbourhood_attention(q, k, v, k_ctx, v_ctx, P['rpb'])
    y = merge_branches(o_a, u, vb, ga, gb, P)
    x = x + g1 * rms_norm(y, P['n_mix_post'])
    return ffn_residual(x, sh2, sc2, g2, P)


def setup_inputs(seed: int = 0) -> dict:
    key = jax.random.key(seed)
    ks = jax.random.split(key, 24)
    nrm = jax.random.normal
    f32 = jnp.float32
    return {
        "x_prompt": nrm(ks[0], (BATCH, SEQ, D_MODEL), f32),
        "x_sample": nrm(ks[1], (DEC_BATCH, DEC_SEQ, D_MODEL), f32),
        "cache_k": nrm(ks[2], (DEC_BATCH, DEPTH, PAST_LEN, N_HEADS_A, HEAD_DIM), f32),
        "cache_v": nrm(ks[3], (DEC_BATCH, DEPTH, PAST_LEN, N_HEADS_A, HEAD_DIM), f32),
        "c": nrm(ks[4], (DEC_BATCH, D_MODEL), f32),
        "c_ctx": nrm(ks[5], (D_MODEL,), f32),
        "w_ada": nrm(ks[6], (DEPTH, D_MODEL, N_MOD * D_MODEL), f32) * (0.5 * D_MODEL ** -0.5),
        "b_ada": nrm(ks[7], (DEPTH, N_MOD * D_MODEL), f32) * 0.02,
        "norm_mix_pre": 1.0 + 0.02 * nrm(ks[8], (DEPTH, D_MODEL), f32),
        "norm_mix_post": 1.0 + 0.02 * nrm(ks[9], (DEPTH, D_MODEL), f32),
        "norm_ffn_pre": 1.0 + 0.02 * nrm(ks[10], (DEPTH, D_MODEL), f32),
        "norm_ffn_post": 1.0 + 0.02 * nrm(ks[11], (DEPTH, D_MODEL), f32),
        "w_in": nrm(ks[12], (DEPTH, D_MODEL, D_IN), f32) * D_MODEL ** -0.5,
        "rpb": nrm(ks[13], (DEPTH, N_HEADS_A, 2 * KH_MAX - 1, 2 * KW - 1), f32) * 0.2,
        "ln_v": 1.0 + 0.02 * nrm(ks[14], (DEPTH, D_GMLP), f32),
        "w_s": nrm(ks[15], (DEPTH, N_GROUPS_B, CHUNK, CHUNK), f32) * CHUNK ** -0.5,
        "b_s": 1.0 + 0.02 * nrm(ks[16], (DEPTH, N_GROUPS_B, CHUNK), f32),
        "w_pa": nrm(ks[17], (DEPTH, D_ATTN, D_MODEL), f32) * D_ATTN ** -0.5,
        "w_pb": nrm(ks[18], (DEPTH, D_GMLP, D_MODEL), f32) * D_GMLP ** -0.5,
        "w_o": nrm(ks[19], (DEPTH, D_MODEL, D_MODEL), f32) * D_MODEL ** -0.5,
        "w_gate": nrm(ks[20], (DEPTH, D_MODEL, D_FF), f32) * D_MODEL ** -0.5,
        "w_up": nrm(ks[21], (DEPTH, D_MODEL, D_FF), f32) * D_MODEL ** -0.5,
        "w_down": nrm(ks[22], (DEPTH, D_FF, D_MODEL), f32) * D_FF ** -0.5,
    }


def reference(x_prompt, x_sample, cache_k, cache_v, c, c_ctx, w_ada, b_ada,
              norm_mix_pre, norm_mix_post, norm_ffn_pre, norm_ffn_post, w_in, rpb,
              ln_v, w_s, b_s, w_pa, w_pb, w_o, w_gate, w_up, w_down):
    y_prompt = x_prompt
    y_sample = x_sample
    new_k = []
    new_v = []
    for l in range(DEPTH):
        P = {
            'n_mix_pre': norm_mix_pre[l], 'n_mix_post': norm_mix_post[l],
            'n_ffn_pre': norm_ffn_pre[l], 'n_ffn_post': norm_ffn_post[l],
            'w_in': w_in[l], 'rpb': rpb[l], 'ln_v': ln_v[l], 'w_s': w_s[l], 'b_s': b_s[l],
            'w_pa': w_pa[l], 'w_pb': w_pb[l], 'w_o': w_o[l],
            'w_gate': w_gate[l], 'w_up': w_up[l], 'w_down': w_down[l],
        }
        mods_ctx = modulation(c_ctx, w_ada[l], b_ada[l])
        mods_lat = modulation(c, w_ada[l], b_ada[l])
        y_prompt, k_l, v_l = context_layer(y_prompt, mods_ctx, P)
        new_k.append(k_l)
        new_v.append(v_l)
        y_sample = latent_layer(y_sample, mods_lat, cache_k[:, l], cache_v[:, l], P)
    state_k = jnp.stack(new_k, axis=1)
    state_v = jnp.stack(new_v, axis=1)
    return (y_prompt, y_sample, state_k, state_v)
```

```python
import contextlib
import numpy as np
import concourse.bass as bass
import concourse.mybir as mybir
from concourse.bass_utils import run_bass_kernel_spmd

F32 = mybir.dt.float32
BF16 = mybir.dt.bfloat16
AF = mybir.ActivationFunctionType
ALU = mybir.AluOpType

D = 2048
NH = 8
DH = 128
DFF = 5632
DIN = 9216
NKC = 16
EPS = 1e-6
ATTN_SCALE = DH ** -0.5
NEG = -30000.0
NSLOT = 3
SLOT_ELEMS = 8192
SB_BYTES = 207 * 1024
DEBUG = False


class Buf:
    __slots__ = ("name", "w", "r", "excl", "dsem", "dcnt", "ssem", "scnt")

    def __init__(self, name, excl=False, inherit=None):
        self.name = name
        self.w = {}
        self.r = dict(inherit) if inherit else {}
        self.excl = excl
        self.dsem = None
        self.dcnt = 0
        self.ssem = None
        self.scnt = 0


def _merge(d, ev):
    for k, (s, v) in ev.items():
        cur = d.get(k)
        if cur is None or cur[1] < v:
            d[k] = (s, v)


class Engine:
    def __init__(self, tr, name, e, own_sem=True):
        self.tr = tr
        self.name = name
        self.e = e
        self.sem = tr.new_sem("s_" + name) if own_sem else None
        self.cnt = 0
        self.waited = {}

    def wait_ev(self, s, v):
        k = s.num
        if self.waited.get(k, 0) >= v:
            return
        self.e.wait_ge(s, v)
        self.waited[k] = v

    def wait_deps(self, reads, writes, skip_own=False):
        d = {}
        for b in reads:
            _merge(d, b.w)
            if b.excl:
                _merge(d, b.r)
        for b in writes:
            _merge(d, b.w)
            _merge(d, b.r)
        for k, (s, v) in d.items():
            if skip_own and self.sem is not None and s.num == self.sem.num:
                continue
            self.wait_ev(s, v)

    def signal(self, ins):
        self.cnt += 1
        ins.then_inc(self.sem, 1)
        return (self.sem, self.cnt)


def _commit(ev, reads, writes):
    s, v = ev
    k = s.num
    wset = set(id(b) for b in writes)
    for b in reads:
        if id(b) in wset:
            continue
        if b.excl:
            b.w = {k: (s, v)}
            b.r = {}
        else:
            cur = b.r.get(k)
            if cur is None or cur[1] < v:
                b.r[k] = (s, v)
    for b in writes:
        b.w = {k: (s, v)}
        b.r = {}


class Tracker:
    def __init__(self, nc, es):
        self.nc = nc
        self.es = es
        self.nsem = 0
        self.free_events = {}

    def new_sem(self, name):
        self.nsem += 1
        return self.es.enter_context(self.nc.semaphore(name))

    def buf(self, name, excl=False):
        return Buf(name, excl=excl, inherit=self.free_events)

    def retire(self, bufs):
        for b in bufs:
            _merge(self.free_events, b.w)
            _merge(self.free_events, b.r)


class SbAlloc:
    def __init__(self, big, nbytes):
        self.big = big
        self.free = [(0, nbytes)]
        self.live = {}

    def alloc(self, name, nbytes):
        nbytes = (nbytes + 63) // 64 * 64
        for i, (o, n) in enumerate(self.free):
            if n >= nbytes:
                if n == nbytes:
                    self.free.pop(i)
                else:
                    self.free[i] = (o + nbytes, n - nbytes)
                self.live[name] = (o, nbytes)
                return o
        raise RuntimeError(f"SBUF alloc failed for {name} ({nbytes} B); free={self.free}")

    def release(self, name):
        o, n = self.live.pop(name)
        self.free.append((o, n))
        self.free.sort()
        merged = []
        for (a, b) in self.free:
            if merged and merged[-1][0] + merged[-1][1] == a:
                merged[-1] = (merged[-1][0], merged[-1][1] + b)
            else:
                merged.append((a, b))
        self.free = merged

    def bf(self, off, n):
        return self.big[:, off // 2: off // 2 + n]

    def f32(self, off, n):
        return self.big[:, off // 2: off // 2 + 2 * n].bitcast(F32)


def build_program():
    nc = bass.Bass("TRN2", target_bir_lowering=False)

    def din(name, shape):
        return nc.dram_tensor(name, list(shape), F32, kind="ExternalInput").ap()

    def dout(name, shape):
        return nc.dram_tensor(name, list(shape), F32, kind="ExternalOutput").ap()

    xin = din("xin", [1280, D])
    ck_d = din("ck", [512, 1024])
    cv_d = din("cv", [512, 1024])
    cvT_d = din("cvT", [128, 16, 2])
    w_ada = din("w_ada", [D, 6 * D])
    b_adaT_d = din("b_adaT", [128, 6, 16])
    nvT_d = din("nvT", [128, 4, 16])
    w_in = din("w_in", [D, DIN])
    biasT_d = din("biasT", [NH, 128, 6, 512])
    lnv_d = din("ln_v", [1, 1024])
    wsT_d = din("w_sT", [128, 8, 128])
    bs_d = din("b_s", [1, 1024])
    w_pa = din("w_pa", [1024, D])
    w_pb = din("w_pb", [1024, D])
    w_o = din("w_o", [D, D])
    w_gate = din("w_gate", [D, DFF])
    w_up = din("w_up", [D, DFF])
    w_down = din("w_down", [DFF, D])
    yp_d = dout("yp", [512, D])
    ys_d = dout("ys", [512, D])
    sk_d = dout("sk", [512, 1024])
    sv_d = dout("sv", [512, 1024])
    x1s = nc.dram_tensor("x1s", [1024, D], F32).ap()
    y2s = nc.dram_tensor("y2s", [1024, D], F32).ap()
    dbg = {}

    es = contextlib.ExitStack()
    with es:
        tr = Tracker(nc, es)
        big = es.enter_context(nc.sbuf_tensor("big", [128, SB_BYTES // 2], BF16))
        sb = SbAlloc(big, SB_BYTES)
        banks_t = [es.enter_context(nc.psum_tensor(f"bank{i}", [128, 512], F32)) for i in range(8)]
        bankB = [Buf(f"bank{i}", excl=True) for i in range(8)]

        PE = Engine(tr, "pe", nc.tensor)
        ACT = Engine(tr, "act", nc.scalar)
        DVE = Engine(tr, "dve", nc.vector)
        POOL = Engine(tr, "pool", nc.gpsimd)
        SP = Engine(tr, "sp", nc.sync, own_sem=False)
        final_events = {}

        def op(eng, reads, writes, fn):
            eng.wait_deps(reads, writes)
            ins = fn()
            ev = eng.signal(ins)
            _commit(ev, reads, writes)
            return ev

        def dma(q, out, in_, reads, writes, dbuf, is_output=False):
            q.wait_deps(reads, writes)
            ins = q.e.dma_start(out=out, in_=in_)
            if q is POOL:
                if dbuf.ssem is None:
                    dbuf.ssem = tr.new_sem("w_" + dbuf.name)
                dbuf.scnt += 1
                ins.then_inc(dbuf.ssem, 16)
                ev = (dbuf.ssem, 16 * dbuf.scnt)
            else:
                if dbuf.dsem is None:
                    dbuf.dsem = tr.new_sem("d_" + dbuf.name)
                dbuf.dcnt += 1
                ins.then_inc(dbuf.dsem, 16)
                ev = (dbuf.dsem, 16 * dbuf.dcnt)
            _commit(ev, reads, writes)
            if is_output:
                _merge(final_events, {ev[0].num: ev})
            return ev

        def pe_begin(reads, writes):
            PE.wait_deps(reads, writes, skip_own=True)

        def pe_end(ins, reads, writes):
            ev = PE.signal(ins)
            _commit(ev, reads, writes)

        class Tile:
            def __init__(self, name, nbytes):
                self.name = name
                self.off = sb.alloc(name, nbytes)
                self.nbytes = nbytes
                self.bufs = []

            def buf(self, suffix=""):
                b = tr.buf(self.name + suffix)
                self.bufs.append(b)
                return b

            def bf(self, n=None, o=0):
                n = (self.nbytes - o) // 2 if n is None else n
                return sb.bf(self.off + o, n)

            def f32(self, n=None, o=0):
                n = (self.nbytes - o) // 4 if n is None else n
                return sb.f32(self.off + o, n)

            def free(self):
                tr.retire(self.bufs)
                sb.release(self.name)

        cst = Tile("cst", 24 * 1024)
        cst_b = cst.buf()
        co = [0]

        def cf32(n):
            v = cst.f32(n, co[0])
            co[0] += 4 * n
            return v

        def cbf(n):
            v = cst.bf(n, co[0])
            co[0] += 2 * n
            return v

        identf = cf32(128)
        identb = cbf(128)
        onesb = cbf(128)
        onesf = cf32(128)
        sel = cf32(128)
        cvT = cf32(32).rearrange("p (k v) -> p k v", v=2)
        csb = cbf(32).rearrange("p (k v) -> p k v", v=2)
        b_adaT = cf32(96).rearrange("p (m c) -> p m c", c=16)
        nvT = cf32(64).rearrange("p (m c) -> p m c", c=16)
        MODS = cf32(192).rearrange("p (m c v) -> p m c v", m=6, c=16, v=2)
        A1 = cf32(32).rearrange("p (c v) -> p c v", v=2)
        A2 = cf32(32).rearrange("p (c v) -> p c v", v=2)
        G1 = cf32(32).rearrange("p (c v) -> p c v", v=2)
        G2 = cf32(32).rearrange("p (c v) -> p c v", v=2)
        LNG = cf32(1024)
        BS_flat = cf32(1024)
        BS = BS_flat.rearrange("p (g q) -> p g q", q=128)
        wsT = cbf(1024).rearrange("p (g q) -> p g q", q=128)
        stats = cf32(512)
        eps_t = cf32(8)
        assert co[0] <= 24 * 1024, co[0]

        csem = tr.new_sem("d_const")
        ncl = 0
        for (o_, i_) in [(cvT, cvT_d), (b_adaT, b_adaT_d), (nvT, nvT_d)]:
            nc.sync.dma_start(out=o_, in_=i_).then_inc(csem, 16)
            ncl += 1
        nc.sync.dma_start(out=LNG, in_=lnv_d[0, :].partition_broadcast(128)).then_inc(csem, 16)
        ncl += 1
        nc.sync.dma_start(out=BS_flat, in_=bs_d[0, :].partition_broadcast(128)).then_inc(csem, 16)
        ncl += 1
        cload_ev = {csem.num: (csem, 16 * ncl)}

        mods_b = tr.buf("mods")
        A_b = tr.buf("A1A2G")
        stat_b = {}

        scol = [0]

        def stat_cols(n):
            c0 = scol[0]
            scol[0] += n
            assert scol[0] <= 512
            return stats[:, c0:c0 + n]

        junk_t = Tile("junk", 2048 * 2)
        junk = junk_t.bf()
        junk_b = [junk_t.buf(f"_{i}") for i in range(4)]
        jctr = [0]

        def jq():
            i = jctr[0] % 4
            jctr[0] += 1
            return junk[:, i * 512:(i + 1) * 512], junk_b[i]
        tmp_t = Tile("tmp", 3 * 512 * 4)
        tmp_b = [tmp_t.buf(f"_{i}") for i in range(3)]
        tmp_v = [tmp_t.f32(512, i * 2048) for i in range(3)]
        tmpc = [0]
        ring_t = Tile("ring", NSLOT * SLOT_ELEMS * 2)
        slot_bufs = [ring_t.buf(f"_s{i}") for i in range(NSLOT)]
        ring = {"sched": [], "issued": 0, "consumed": 0, "released": 0}

        def slot_view(i):
            return ring_t.bf(SLOT_ELEMS, i * SLOT_ELEMS * 2)

        def ring_issue():
            while ring["issued"] < len(ring["sched"]) and ring["issued"] < ring["released"] + NSLOT:
                n = ring["issued"]
                key, parts = ring["sched"][n]
                i = n % NSLOT
                sbf = slot_bufs[i]
                POOL.wait_deps([], [sbf])
                if sbf.dsem is None:
                    sbf.dsem = tr.new_sem("d_" + sbf.name)
                for (src, eo, kc, ncol) in parts:
                    dst = slot_view(i)[:, eo:eo + kc * ncol].rearrange("p (k n) -> p k n", n=ncol)
                    nc.gpsimd.dma_start(out=dst, in_=src).then_inc(sbf.dsem, 16)
                    sbf.dcnt += 1
                _commit((sbf.dsem, 16 * sbf.dcnt), [], [sbf])
                ring["issued"] += 1

        def ring_next(key):
            n = ring["consumed"]
            k2, parts = ring["sched"][n]
            assert k2 == key, (k2, key)
            ring["consumed"] += 1
            i = n % NSLOT
            views = [slot_view(i)[:, eo:eo + kc * ncol].rearrange("p (k n) -> p k n", n=ncol)
                     for (src, eo, kc, ncol) in parts]
            return slot_bufs[i], (views[0] if len(views) == 1 else views)

        def ring_release():
            ring["released"] += 1
            assert ring["released"] <= ring["consumed"]
            ring_issue()

        def wslab(w, kc, c0, ncol, r0=0):
            return w[r0 * 128:(r0 + kc) * 128, c0:c0 + ncol].rearrange("(k p) n -> p k n", p=128)

        S = ring["sched"]
        ada_order = [(1, s) for s in range(4)] + [(0, s) for s in range(4)]
        ada_rest = [(m, s) for m in (2, 4, 3, 5) for s in range(4)]
        for (m, s) in ada_order:
            S.append((("ada", m, s), [(wslab(w_ada, 16, m * D + s * 512, 512), 0, 16, 512)]))
        rest_iter = list(ada_rest)

        def sched_ada(n):
            grp = []
            for _ in range(n):
                m, s = rest_iter.pop(0)
                S.append((("ada", m, s), [(wslab(w_ada, 16, m * D + s * 512, 512), 0, 16, 512)]))
                grp.append((m, s))
            return grp

        a1_plan = []
        for i, c0 in enumerate(range(0, 3072, 512)):
            S.append((("win", c0), [(wslab(w_in, 16, c0, 512), 0, 16, 512)]))
            a1_plan.append(sched_ada(1))
        for h in range(NH):
            S.append((("bias", h), [(biasT_d[h], 0, 6, 512)]))
        a3_plan = []
        for c0 in (4096, 4608, 3072, 3584):
            S.append((("win", c0), [(wslab(w_in, 16, c0, 512), 0, 16, 512)]))
            a3_plan.append(sched_ada(1))
        a4_plan = {}
        for cq in range(4):
            S.append((("win", 5120 + cq * 512), [(wslab(w_in, 16, 5120 + cq * 512, 512), 0, 16, 512)]))
            a4_plan[(cq, 0)] = sched_ada(1) if cq <= 2 else []
            S.append((("wpa", cq), [(wslab(w_pa, 8, cq * 512, 512), 0, 8, 512)]))
            S.append((("win", 7168 + cq * 512), [(wslab(w_in, 16, 7168 + cq * 512, 512), 0, 16, 512)]))
            a4_plan[(cq, 1)] = sched_ada(1) if cq <= 2 else []
            S.append((("wpb", cq), [(wslab(w_pb, 8, cq * 512, 512), 0, 8, 512)]))
        b_plan = []
        for half in range(2):
            for cb in range(4):
                S.append((("wo", half, cb), [(wslab(w_o, 16, cb * 512, 512), 0, 16, 512)]))
            b_plan.append([])
        for cbp in range(22):
            S.append((("wgu", cbp), [(wslab(w_gate, 16, cbp * 256, 256), 0, 16, 256),
                                      (wslab(w_up, 16, cbp * 256, 256), 4096, 16, 256)]))
        assert not rest_iter
        KK = [(0, 16), (16, 16), (32, 12)]
        for cb in range(4):
            for kk, (r0, kc) in enumerate(KK):
                S.append((("wd", cb, kk), [(wslab(w_down, kc, cb * 512, 512, r0=r0), 0, kc, 512)]))
        ring_issue()
        G = nc.gpsimd
        pb_ = [Buf("pinit0"), Buf("pinit1")]
        op(POOL, [], [pb_[0]], lambda: G.memset(identf, 0.0))
        op(POOL, [pb_[0]], [pb_[0]], lambda: G.affine_select(out=identf, in_=identf, pattern=[[-1, 128]],
                                                            compare_op=ALU.not_equal, fill=1.0, base=0, channel_multiplier=1))
        op(POOL, [], [pb_[1]], lambda: G.memset(sel, 0.0))
        op(POOL, [pb_[1]], [pb_[1]], lambda: G.affine_select(out=sel, in_=sel, pattern=[[-1, 128]],
                                                            compare_op=ALU.not_equal, fill=1.0, base=0, channel_multiplier=1))
        G.memset(onesf, 1.0)
        G.memset(stats, 0.0)
        ins = G.memset(eps_t, EPS)
        pool_ev = POOL.signal(ins)
        cst_b.w = {pool_ev[0].num: pool_ev}
        _merge(cst_b.w, cload_ev)
        op(DVE, [cst_b], [], lambda: nc.vector.tensor_copy(out=identb, in_=identf))
        ev = op(DVE, [cst_b], [], lambda: nc.vector.tensor_copy(out=onesb, in_=onesf))
        _merge(cst_b.w, {ev[0].num: ev})
        ev = op(ACT, [cst_b], [], lambda: nc.scalar.activation(out=csb, in_=cvT, func=AF.Silu))
        _merge(cst_b.w, {ev[0].num: ev})
        cst_b.r = {}
        rtile = Tile("rtile", 2 * 2048)
        r_bufs = [rtile.buf("_0"), rtile.buf("_1")]
        kcT_t = Tile("kcT", NH * 512 * 2)
        kcT_b = [kcT_t.buf(f"_{h}") for h in range(NH)]
        kcT = kcT_t.bf().rearrange("p (h t) -> p h t", t=512)
        vc_t = Tile("vc", 4 * 1024 * 2)
        vc_b = vc_t.buf()
        VC = vc_t.bf().rearrange("p (t f) -> p t f", f=1024)
        hT_t = Tile("hT", 16 * 1024 * 2)
        hT = hT_t.bf().rearrange("p (k t) -> p k t", t=1024)
        hT_b = [[hT_t.buf("_Pe"), hT_t.buf("_Po")], [hT_t.buf("_Se"), hT_t.buf("_So")]]
        hTH_t = Tile("hTH", 16 * 256 * 2)
        hTH = hTH_t.bf().rearrange("p (k t) -> p k t", t=256)
        hTH_b = [hTH_t.buf("_e"), hTH_t.buf("_o")]
        KCa = junk.rearrange("p (t f) -> p t f", f=1024)
        KCb = tmp_t.bf(2048).rearrange("p (t f) -> p t f", f=1024)
        xg_t = [Tile("xg0", 4 * D * 4), Tile("xg1", 4 * D * 4)]
        xg_b = [xg_t[0].buf(), xg_t[1].buf()]
        xg_v = [t_.f32().rearrange("p (t d) -> p t d", d=D) for t_ in xg_t]

        def load_x(i, row0, ntile):
            dma(SP, xg_v[i][:, 0:ntile, :], xin[row0:row0 + ntile * 128, :].rearrange("(t p) d -> p t d", p=128),
                [], [xg_b[i]], xg_b[i])


        ada_ctr = [0]

        def do_ada_slab(m, s):
            n = ada_ctr[0]
            ada_ctr[0] += 1
            sbuf_, view = ring_next(("ada", m, s))
            bk = n % 2
            pe_begin([cst_b, sbuf_], [bankB[bk]])
            for k in range(16):
                ins = nc.tensor.matmul(banks_t[bk][0:2, :], lhsT=csb[:, k, :], rhs=view[:, k, :],
                                       start=(k == 0), stop=(k == 15))
            pe_end(ins, [cst_b, sbuf_], [bankB[bk]])
            ring_release()
            rb = r_bufs[n % 2]
            R = rtile.f32(512, (n % 2) * 2048)[0:2, :]
            op(DVE, [bankB[bk]], [rb], lambda: nc.vector.tensor_copy(out=R, in_=banks_t[bk][0:2, :]))
            pe_begin([rb, cst_b], [bankB[2]])
            for j in range(4):
                ins = nc.tensor.matmul(banks_t[2][:, 2 * j:2 * j + 2], lhsT=R[:, j * 128:(j + 1) * 128],
                                       rhs=sel[0:2, 0:2], start=True, stop=True)
            pe_end(ins, [rb, cst_b], [bankB[2]])
            for v in range(2):
                src = banks_t[2][:, 0:8].rearrange("p (j v) -> p j v", v=2)[:, :, v]
                op(DVE, [bankB[2], cst_b], [mods_b],
                   lambda: nc.vector.tensor_tensor(out=MODS[:, m, 4 * s:4 * s + 4, v], in0=src,
                                                   in1=b_adaT[:, m, 4 * s:4 * s + 4], op=ALU.add))

        def finish_mod(kind):
            for v in range(2):
                if kind == 1:
                    op(DVE, [mods_b, cst_b], [A_b], lambda: nc.vector.scalar_tensor_tensor(
                        out=A1[:, :, v], in0=MODS[:, 1, :, v], scalar=1.0, in1=nvT[:, 0, :], op0=ALU.add, op1=ALU.mult))
                elif kind == 2:
                    op(DVE, [mods_b, cst_b], [A_b], lambda: nc.vector.scalar_tensor_tensor(
                        out=A2[:, :, v], in0=MODS[:, 4, :, v], scalar=1.0, in1=nvT[:, 2, :], op0=ALU.add, op1=ALU.mult))
                elif kind == 3:
                    op(DVE, [mods_b, cst_b], [A_b], lambda: nc.vector.tensor_tensor(
                        out=G1[:, :, v], in0=MODS[:, 2, :, v], in1=nvT[:, 1, :], op=ALU.mult))
                else:
                    op(DVE, [mods_b, cst_b], [A_b], lambda: nc.vector.tensor_tensor(
                        out=G2[:, :, v], in0=MODS[:, 5, :, v], in1=nvT[:, 3, :], op=ALU.mult))

        def norm_stats(xg, xg_b, ntile):
            ss = stat_cols(ntile)
            rs = stat_cols(ntile)
            sb_ = tr.buf("st")
            for t in range(ntile):
                op(ACT, [xg_b], [sb_] + junk_b, lambda: nc.scalar.activation(
                    out=junk, in_=xg[:, t, :], func=AF.Square, accum_out=ss[:, t:t + 1]))
            op(DVE, [sb_], [sb_], lambda: nc.vector.tensor_scalar(
                out=rs, in0=ss, scalar1=1.0 / D, scalar2=EPS, op0=ALU.mult, op1=ALU.add))
            op(ACT, [sb_], [sb_], lambda: nc.scalar.activation(out=rs, in_=rs, func=AF.Sqrt))
            op(DVE, [sb_], [sb_], lambda: nc.vector.reciprocal(out=rs, in_=rs))
            for t in range(ntile):
                op(DVE, [sb_, xg_b], [xg_b], lambda: nc.vector.tensor_scalar(
                    out=xg[:, t, :], in0=xg[:, t, :], scalar1=rs[:, t:t + 1], scalar2=None, op0=ALU.mult))

        def transpose_mod(xg, xg_b, ntile, Amod, Bmod, v, dstT, dst_b, dst_tok0, banks, diag=None, diag_b=None, raw=False):
            n = ntile * 128
            for c in range(NKC):
                bk = banks[c % len(banks)]
                rd = [xg_b, cst_b] + ([diag_b] if diag is not None else [])
                pe_begin(rd, [bankB[bk]])
                for t in range(ntile):
                    if diag is None:
                        ins = nc.tensor.transpose(out=banks_t[bk][:, t * 128:(t + 1) * 128],
                                                  in_=xg[:, t, c * 128:(c + 1) * 128], identity=identf)
                    else:
                        ins = nc.tensor.matmul(banks_t[bk][:, t * 128:(t + 1) * 128], lhsT=xg[:, t, c * 128:(c + 1) * 128],
                                               rhs=diag[:, t, :], start=True, stop=True)
                pe_end(ins, rd, [bankB[bk]])
                dst = dstT[:, c, dst_tok0:dst_tok0 + n]
                src = banks_t[bk][:, 0:n]
                db = dst_b[c % 2]
                if raw:
                    if c % 2 == 0:
                        op(ACT, [bankB[bk]], [db], lambda: nc.scalar.copy(out=dst, in_=src))
                    else:
                        op(DVE, [bankB[bk]], [db], lambda: nc.vector.tensor_copy(out=dst, in_=src))
                elif c % 2 == 0:
                    op(ACT, [bankB[bk], A_b, mods_b], [db], lambda: nc.scalar.activation(
                        out=dst, in_=src, func=AF.Identity, scale=Amod[:, c, v:v + 1], bias=Bmod[:, c, v:v + 1]))
                else:
                    op(DVE, [bankB[bk], A_b, mods_b], [db], lambda: nc.vector.tensor_scalar(
                        out=dst, in0=src, scalar1=Amod[:, c, v:v + 1], scalar2=Bmod[:, c, v:v + 1], op0=ALU.mult, op1=ALU.add))

        def modulate(dstT, dst_b, tok0, n, Amod, Bmod, v):
            for c in range(NKC):
                ap = dstT[:, c, tok0:tok0 + n]
                db = dst_b[c % 2]
                if c % 2 == 0:
                    op(ACT, [A_b, mods_b], [db], lambda: nc.scalar.activation(
                        out=ap, in_=ap, func=AF.Identity, scale=Amod[:, c, v:v + 1], bias=Bmod[:, c, v:v + 1]))
                else:
                    op(DVE, [A_b, mods_b], [db], lambda: nc.vector.tensor_scalar(
                        out=ap, in0=ap, scalar1=Amod[:, c, v:v + 1], scalar2=Bmod[:, c, v:v + 1], op0=ALU.mult, op1=ALU.add))

        def norm_transpose(xg, xg_b, ntile, Amod, Bmod, v, dstT, dst_b, dst_tok0, banks):
            norm_stats(xg, xg_b, ntile)
            transpose_mod(xg, xg_b, ntile, Amod, Bmod, v, dstT, dst_b, dst_tok0, banks)

        def kct_load():
            ckv = ck_d.rearrange("(t p) f -> p t f", p=128)
            dma(POOL, KCa, ckv[:, 0:2, :], [], junk_b, junk_b[0])
            dma(POOL, KCb, ckv[:, 2:4, :], [], tmp_b, tmp_b[0])

        def kct_prep():
            for h in range(NH):
                bk = 3 + (h % 4)
                rd = junk_b + tmp_b + [cst_b]
                pe_begin(rd, [bankB[bk]])
                for t in range(4):
                    src = KCa[:, t, h * 128:(h + 1) * 128] if t < 2 else KCb[:, t - 2, h * 128:(h + 1) * 128]
                    ins = nc.tensor.matmul(banks_t[bk][:, t * 128:(t + 1) * 128], lhsT=src, rhs=identb, start=True, stop=True)
                pe_end(ins, rd, [bankB[bk]])
                if h % 2 == 0:
                    op(ACT, [bankB[bk]], [kcT_b[h]], lambda: nc.scalar.copy(out=kcT[:, h, :], in_=banks_t[bk][:, :]))
                else:
                    op(DVE, [bankB[bk]], [kcT_b[h]], lambda: nc.vector.tensor_copy(out=kcT[:, h, :], in_=banks_t[bk][:, :]))

        B1 = MODS[:, 0]
        a0b = [3, 4, 5, 6, 7]
        load_x(0, 0, 4)
        load_x(1, 512, 4)
        for ai, (m, s) in enumerate(ada_order):
            do_ada_slab(m, s)
            if ai == 0:
                norm_stats(xg_v[0], xg_b[0], 4)
            if ai == 1:
                norm_stats(xg_v[1], xg_b[1], 4)
            if ai == 3:
                transpose_mod(xg_v[0], xg_b[0], 4, None, None, 0, hT, hT_b[0], 0, a0b, raw=True)
                load_x(0, 1024, 2)
            if ai == 4:
                transpose_mod(xg_v[1], xg_b[1], 4, None, None, 1, hT, hT_b[1], 512, a0b, raw=True)
                norm_stats(xg_v[0], xg_b[0], 2)
            if ai == 5:
                transpose_mod(xg_v[0], xg_b[0], 2, None, None, 1, hTH, hTH_b, 0, a0b, raw=True)
        finish_mod(1)
        kct_load()
        dma(POOL, VC, cv_d.rearrange("(t p) f -> p t f", p=128), [], [vc_b], vc_b)
        modulate(hT, hT_b[0], 0, 512, A1, B1, 0)
        modulate(hT, hT_b[1], 512, 512, A1, B1, 1)
        modulate(hTH, hTH_b, 0, 256, A1, B1, 1)
        xg_t[1].free()
        xg_t[0].free()

        qT_t = Tile("qT", NH * 1024 * 2)
        qT = qT_t.bf().rearrange("p (h t) -> p h t", t=1024)
        q_b = {}
        for h in range(NH):
            q_b[(h, 0)] = qT_t.buf(f"_{h}p0")
            q_b[(h, 1)] = qT_t.buf(f"_{h}p1")
            q_b[(h, 2)] = qT_t.buf(f"_{h}s")
        kT_t = Tile("kT", NH * 1280 * 2)
        kT = kT_t.bf().rearrange("p (h t) -> p h t", t=1280)
        k_b = {(h, g): kT_t.buf(f"_{h}{g}") for h in range(NH) for g in range(3)}
        v_t = Tile("v", 10 * 1024 * 2)
        Vt = v_t.bf().rearrange("p (t f) -> p t f", f=1024)
        v_b = [v_t.buf(f"_{t}") for t in range(10)]
        stg_t = Tile("stg", 3 * 512 * 4)
        stg_b = [stg_t.buf(f"_{i}") for i in range(3)]
        stg_v = [stg_t.f32(512, i * 2048) for i in range(3)]
        stg_ctr = [0]
        a1_banks = [3, 4, 5, 6, 7]
        bctr = [0]

        def nb(lst):
            b = lst[bctr[0] % len(lst)]
            bctr[0] += 1
            return b

        evac_ctr = [0]

        def evac_copy(bk, out_ap, out_b, scale=None, ncols=512):
            src = banks_t[bk][:, 0:ncols]
            use_act = (evac_ctr[0] % 2 == 0)
            evac_ctr[0] += 1
            if use_act:
                if scale is None:
                    op(ACT, [bankB[bk]], [out_b], lambda: nc.scalar.copy(out=out_ap, in_=src))
                else:
                    op(ACT, [bankB[bk]], [out_b], lambda: nc.scalar.mul(out=out_ap, in_=src, mul=scale))
            else:
                if scale is None:
                    op(DVE, [bankB[bk]], [out_b], lambda: nc.vector.tensor_copy(out=out_ap, in_=src))
                else:
                    op(DVE, [bankB[bk]], [out_b], lambda: nc.vector.tensor_scalar(
                        out=out_ap, in0=src, scalar1=scale, scalar2=None, op0=ALU.mult))

        def fm_group(bk, slab_b, view, j, rhsT, rhs_b, tok0, ntok, nk=16):
            rd = [slab_b] + list(rhs_b)
            pe_begin(rd, [bankB[bk]])
            for k in range(nk):
                ins = nc.tensor.matmul(banks_t[bk][:, 0:ntok], lhsT=view[:, k, j * 128:(j + 1) * 128],
                                       rhs=rhsT[:, k, tok0:tok0 + ntok], start=(k == 0), stop=(k == nk - 1))
            pe_end(ins, rd, [bankB[bk]])

        def tm_group(bk, slab_b, view, lhsT_all, lhs_b, tok0, nk=16, ncol=512):
            rd = [slab_b] + list(lhs_b)
            pe_begin(rd, [bankB[bk]])
            for k in range(nk):
                ins = nc.tensor.matmul(banks_t[bk][:, 0:ncol], lhsT=lhsT_all[:, k, tok0:tok0 + 128],
                                       rhs=view[:, k, 0:ncol], start=(k == 0), stop=(k == nk - 1))
            pe_end(ins, rd, [bankB[bk]])

        def store_state(bk, dst_d, row0, col0):
            i = stg_ctr[0] % 3
            stg_ctr[0] += 1
            op(DVE, [bankB[bk]], [stg_b[i]], lambda: nc.vector.tensor_copy(out=stg_v[i], in_=banks_t[bk][:, :]))
            dma(SP, dst_d[row0:row0 + 128, col0:col0 + 512], stg_v[i], [stg_b[i]], [], stg_b[i], is_output=True)

        for si, c0 in enumerate(range(0, 3072, 512)):
            slab_b, view = ring_next(("win", c0))
            if c0 < 1024:
                for tb in range(2):
                    for j in range(4):
                        h = (c0 // 512) * 4 + j
                        bk = nb(a1_banks)
                        fm_group(bk, slab_b, view, j, hT, hT_b[tb], tb * 512, 512)
                        if tb == 0:
                            use = [q_b[(h, 0)], q_b[(h, 1)]]
                            src = banks_t[bk][:, :]
                            op(DVE, [bankB[bk]], use, lambda: nc.vector.tensor_scalar(
                                out=qT[:, h, 0:512], in0=src, scalar1=ATTN_SCALE, scalar2=None, op0=ALU.mult))
                        else:
                            evac_copy(bk, qT[:, h, 512:1024], q_b[(h, 2)], scale=ATTN_SCALE)
            elif c0 < 2048:
                cc = c0 - 1024
                for j in range(4):
                    h = (cc // 512) * 4 + j
                    for g, (src_T, src_b, t0, nt) in [(1, (hT, hT_b[1], 512, 512)), (2, (hTH, hTH_b, 0, 256))]:
                        bk = nb(a1_banks)
                        fm_group(bk, slab_b, view, j, src_T, src_b, t0, nt)
                        evac_copy(bk, kT[:, h, g * 512:g * 512 + nt], k_b[(h, g)], ncols=nt)
                kb16 = tmp_t.bf(2048).rearrange("p (t f) -> p t f", f=512)
                for t in range(4):
                    bk = nb(a1_banks)
                    tm_group(bk, slab_b, view, hT, hT_b[0], t * 128)
                    op(ACT, [bankB[bk]], tmp_b, lambda: nc.scalar.copy(out=kb16[:, t, :], in_=banks_t[bk][:, :]))
                    store_state(bk, sk_d, t * 128, cc)
                for j in range(4):
                    h = (cc // 512) * 4 + j
                    bk = nb(a1_banks)
                    pe_begin(tmp_b + [cst_b], [bankB[bk]])
                    for t in range(4):
                        ins = nc.tensor.matmul(banks_t[bk][:, t * 128:(t + 1) * 128], lhsT=kb16[:, t, j * 128:(j + 1) * 128],
                                               rhs=identb, start=True, stop=True)
                    pe_end(ins, tmp_b + [cst_b], [bankB[bk]])
                    evac_copy(bk, kT[:, h, 0:512], k_b[(h, 0)])
            else:
                cc = c0 - 2048
                for t in range(10):
                    bk = nb(a1_banks)
                    if t < 8:
                        tm_group(bk, slab_b, view, hT, hT_b[t // 4], t * 128)
                    else:
                        tm_group(bk, slab_b, view, hTH, hTH_b, (t - 8) * 128)
                    op(ACT, [bankB[bk]], [v_b[t]], lambda: nc.scalar.copy(
                        out=Vt[:, t, cc:cc + 512], in_=banks_t[bk][:, :]))
                    if t < 4:
                        store_state(bk, sv_d, t * 128, cc)
            ring_release()
            for (m, s) in a1_plan[si]:
                do_ada_slab(m, s)
            if si == 0:
                kct_prep()
        hTH_t.free()

        pt_t = Tile("pt", 3 * 512 * 2)
        pt_b = [pt_t.buf(f"_{i}") for i in range(3)]
        pt_v = [pt_t.bf(512, i * 1024) for i in range(3)]
        rc_t = Tile("rc", 2 * 512 * 4)
        rc_b = [rc_t.buf("_0"), rc_t.buf("_1")]
        rc_v = [rc_t.f32(512, i * 2048) for i in range(2)]
        st_banks = [0, 1]
        unit = [0]
        ptc = [0]

        def attn_unit(h, qbuf, q_ap, nq, ktiles, oa_ap):
            u = unit[0]
            unit[0] += 1
            ob = 2 + 2 * (u % 3)
            sb_ = 3 + 2 * (u % 3)
            n = len(ktiles)
            pend = None

            def emit_pv(idx, pi):
                (ka, kb_, va, vb_, ba, bb_) = ktiles[idx]
                pe_begin([pt_b[pi], vb_, cst_b], [bankB[ob], bankB[sb_]])
                nc.tensor.matmul(banks_t[ob][:, 0:nq], lhsT=va, rhs=pt_v[pi][:, 0:nq], start=(idx == 0), stop=(idx == n - 1))
                ins = nc.tensor.matmul(banks_t[sb_][:, 0:nq], lhsT=onesb, rhs=pt_v[pi][:, 0:nq], start=(idx == 0),
                                       stop=(idx == n - 1))
                pe_end(ins, [pt_b[pi], vb_, cst_b], [bankB[ob], bankB[sb_]])

            for idx in range(n):
                (ka, kb_, va, vb_, ba, bb_) = ktiles[idx]
                bk = nb(st_banks)
                rd = [kb_, qbuf, cst_b] + ([bb_] if ba is not None else [])
                pe_begin(rd, [bankB[bk]])
                ins = nc.tensor.matmul(banks_t[bk][:, 0:nq], lhsT=ka, rhs=q_ap, start=True, stop=(ba is None))
                if ba is not None:
                    ins = nc.tensor.matmul(banks_t[bk][:, 0:nq], lhsT=identb, rhs=ba, start=False, stop=True)
                pe_end(ins, rd, [bankB[bk]])
                pi = ptc[0] % 3
                ptc[0] += 1
                op(ACT, [bankB[bk]], [pt_b[pi]], lambda: nc.scalar.activation(
                    out=pt_v[pi][:, 0:nq], in_=banks_t[bk][:, 0:nq], func=AF.Exp))
                if pend is not None:
                    emit_pv(*pend)
                pend = (idx, pi)
            emit_pv(*pend)
            ri = u % 2
            op(DVE, [bankB[sb_]], [rc_b[ri]], lambda: nc.vector.reciprocal(out=rc_v[ri][:, 0:nq], in_=banks_t[sb_][:, 0:nq]))
            op(DVE, [bankB[ob], rc_b[ri]], [qbuf], lambda: nc.vector.tensor_tensor(
                out=oa_ap, in0=banks_t[ob][:, 0:nq], in1=rc_v[ri][:, 0:nq], op=ALU.mult))

        def ctx_unit(s_, h):
            kts = []
            for kt in range(2):
                tk0 = s_ * 256 + kt * 128
                kts.append((kT[:, h, tk0:tk0 + 128], k_b[(h, 0)], Vt[:, s_ * 2 + kt, h * 128:(h + 1) * 128],
                            v_b[s_ * 2 + kt], None, None))
            attn_unit(h, q_b[(h, s_)], qT[:, h, s_ * 256:(s_ + 1) * 256], 256, kts, qT[:, h, s_ * 256:(s_ + 1) * 256])

        for h in range(NH):
            bias_b, bview = ring_next(("bias", h))
            kts = []
            for j in range(6):
                g = 1 if j < 4 else 2
                tk0 = 512 + j * 128
                kts.append((kT[:, h, tk0:tk0 + 128], k_b[(h, g)], Vt[:, 4 + j, h * 128:(h + 1) * 128], v_b[4 + j],
                            bview[:, j, :], bias_b))
            for j in range(4):
                kts.append((kcT[:, h, j * 128:(j + 1) * 128], kcT_b[h], VC[:, j, h * 128:(h + 1) * 128], vc_b, None, None))
            attn_unit(h, q_b[(h, 2)], qT[:, h, 512:1024], 512, kts, qT[:, h, 512:1024])
            ring_release()
            ctx_unit(0, h)
            ctx_unit(1, h)
        oaT = qT
        oa_b = [tr.buf("oaP"), tr.buf("oaS")]
        for h in range(NH):
            for g_ in range(3):
                _merge(oa_b[0 if g_ < 2 else 1].w, q_b[(h, g_)].w)
        rc_t.free()
        pt_t.free()
        stg_t.free()
        v_t.free()
        kT_t.free()
        vc_t.free()
        kcT_t.free()

        gv_t = Tile("gv", 8 * 1024 * 2)
        GV = gv_t.bf().rearrange("p (t f) -> p t f", f=1024)
        gv_b = [gv_t.buf(f"_{t}") for t in range(8)]
        gu_t = Tile("gu", 8 * 1024 * 2)
        guT = gu_t.bf().rearrange("p (c t) -> p c t", t=1024)
        gu_b = {(c, tb): gu_t.buf(f"_{c}{tb}") for c in range(8) for tb in range(2)}
        allb = list(range(8))
        s1 = stat_cols(16).rearrange("p (t s) -> p t s", s=2)
        s2 = stat_cols(16).rearrange("p (t s) -> p t s", s=2)
        lst = stat_cols(40).rearrange("p (a t) -> p a t", t=8)
        ln_b = tr.buf("lnstat")
        ws_b = tr.buf("wsT")
        dma(POOL, wsT, wsT_d, [], [ws_b], ws_b)
        for si, c0 in enumerate((4096, 4608)):
            slab_b, view = ring_next(("win", c0))
            cc = c0 - 4096
            for t in range(8):
                bk = nb(allb)
                tm_group(bk, slab_b, view, hT, hT_b[t // 4], t * 128)
                op(ACT, [bankB[bk]], [gv_b[t], ln_b], lambda: nc.scalar.activation(
                    out=GV[:, t, cc:cc + 512], in_=banks_t[bk][:, :], func=AF.Gelu_apprx_tanh,
                    accum_out=s1[:, t, si:si + 1]))
                i = tmpc[0] % 3
                tmpc[0] += 1
                op(DVE, [gv_b[t]], [tmp_b[i], ln_b], lambda: nc.vector.tensor_tensor(
                    out=tmp_v[i], in0=GV[:, t, cc:cc + 512], in1=GV[:, t, cc:cc + 512], op=ALU.mult))
                jv, jb = jq()
                op(ACT, [tmp_b[i]], [ln_b, jb], lambda: nc.scalar.activation(
                    out=jv, in_=tmp_v[i], func=AF.Identity, accum_out=s2[:, t, si:si + 1]))
            ring_release()
            for (m, s) in a3_plan[si]:
                do_ada_slab(m, s)
        mean, ex2, var, rstd_ln = lst[:, 0, :], lst[:, 1, :], lst[:, 2, :], lst[:, 3, :]
        op(DVE, [ln_b], [ln_b], lambda: nc.vector.tensor_tensor(out=mean, in0=s1[:, :, 0], in1=s1[:, :, 1], op=ALU.add))
        op(DVE, [ln_b], [ln_b], lambda: nc.vector.tensor_tensor(out=ex2, in0=s2[:, :, 0], in1=s2[:, :, 1], op=ALU.add))
        op(DVE, [ln_b], [ln_b], lambda: nc.vector.tensor_scalar(out=mean, in0=mean, scalar1=1.0 / 1024, scalar2=None, op0=ALU.mult))
        op(DVE, [ln_b], [ln_b], lambda: nc.vector.tensor_tensor(out=var, in0=mean, in1=mean, op=ALU.mult))
        op(DVE, [ln_b], [ln_b], lambda: nc.vector.scalar_tensor_tensor(
            out=var, in0=ex2, scalar=1.0 / 1024, in1=var, op0=ALU.mult, op1=ALU.subtract))
        op(DVE, [ln_b], [ln_b], lambda: nc.vector.tensor_scalar(out=var, in0=var, scalar1=EPS, scalar2=None, op0=ALU.add))
        op(ACT, [ln_b], [ln_b], lambda: nc.scalar.activation(out=rstd_ln, in_=var, func=AF.Sqrt))
        op(DVE, [ln_b], [ln_b], lambda: nc.vector.reciprocal(out=rstd_ln, in_=rstd_ln))
        for t in range(8):
            for hf in range(2):
                i = tmpc[0] % 3
                tmpc[0] += 1
                op(DVE, [gv_b[t], ln_b], [tmp_b[i]], lambda: nc.vector.tensor_scalar(
                    out=tmp_v[i], in0=GV[:, t, hf * 512:(hf + 1) * 512], scalar1=mean[:, t:t + 1],
                    scalar2=rstd_ln[:, t:t + 1], op0=ALU.subtract, op1=ALU.mult))
                op(DVE, [tmp_b[i], cst_b], [gv_b[t]], lambda: nc.vector.tensor_tensor(
                    out=GV[:, t, hf * 512:(hf + 1) * 512], in0=tmp_v[i], in1=LNG[:, hf * 512:(hf + 1) * 512], op=ALU.mult))
        def gate(gh):
            for t in range(8):
                bk = nb(allb)
                pe_begin([gv_b[t], ws_b], [bankB[bk]])
                for gg in range(4):
                    g = gh * 4 + gg
                    ins = nc.tensor.matmul(banks_t[bk][:, gg * 128:(gg + 1) * 128], lhsT=GV[:, t, g * 128:(g + 1) * 128],
                                           rhs=wsT[:, g, :], start=True, stop=True)
                pe_end(ins, [gv_b[t], ws_b], [bankB[bk]])
                i = tmpc[0] % 3
                tmpc[0] += 1
                tv = tmp_v[i].rearrange("p (g q) -> p g q", q=128)
                op(DVE, [bankB[bk], cst_b], [tmp_b[i]], lambda: nc.vector.tensor_tensor(
                    out=tv, in0=banks_t[bk][:, :].rearrange("p (g q) -> p g q", q=128), in1=BS[:, gh * 4:gh * 4 + 4, :], op=ALU.add))
                gbs = [gu_b[(gh * 4 + gg, t // 4)] for gg in range(4)]
                dst = guT[:, gh * 4:gh * 4 + 4, t * 128:(t + 1) * 128]
                op(DVE, [tmp_b[i]], gbs, lambda: nc.vector.tensor_tensor(out=dst, in0=tv, in1=dst, op=ALU.mult))

        for si, c0 in enumerate((3072, 3584)):
            slab_b, view = ring_next(("win", c0))
            for j in range(4):
                c = si * 4 + j
                for tb in range(2):
                    bk = nb(allb)
                    fm_group(bk, slab_b, view, j, hT, hT_b[tb], tb * 512, 512)
                    op(ACT, [bankB[bk]], [gu_b[(c, tb)]], lambda: nc.scalar.activation(
                        out=guT[:, c, tb * 512:(tb + 1) * 512], in_=banks_t[bk][:, :], func=AF.Gelu_apprx_tanh))
            ring_release()
            for (m, s) in a3_plan[2 + si]:
                do_ada_slab(m, s)
            gate(si)
        obT = guT
        gv_t.free()

        mT_t = Tile("mT", 16 * 1024 * 2)
        mT = mT_t.bf().rearrange("p (c t) -> p c t", t=1024)
        m_b = [mT_t.buf("_P"), mT_t.buf("_S")]
        sga_t = Tile("sga", 4 * 1024 * 2)
        SGA = sga_t.bf().rearrange("p (c t) -> p c t", t=1024)
        sga_b = {(j, tb): sga_t.buf(f"_{j}{tb}") for j in range(4) for tb in range(2)}
        sgb_t = Tile("sgb", 4 * 1024 * 2)
        SGB = sgb_t.bf().rearrange("p (c t) -> p c t", t=1024)
        sgb_b = {(j, tb): sgb_t.buf(f"_{j}{tb}") for j in range(4) for tb in range(2)}
        ob_all = [gu_b[(c, tb)] for c in range(8) for tb in range(2)]
        ob_tb = [[gu_b[(c, tb)] for c in range(8)] for tb in range(2)]
        for cq in range(4):
            slab_b, view = ring_next(("win", 5120 + cq * 512))
            for j in range(4):
                for tb in range(2):
                    bk = nb(allb)
                    fm_group(bk, slab_b, view, j, hT, hT_b[tb], tb * 512, 512)
                    op(ACT, [bankB[bk]], [sga_b[(j, tb)]], lambda: nc.scalar.activation(
                        out=SGA[:, j, tb * 512:(tb + 1) * 512], in_=banks_t[bk][:, :], func=AF.Sigmoid))
            ring_release()
            for (m, s) in a4_plan[(cq, 0)]:
                do_ada_slab(m, s)
            slab_b, view = ring_next(("wpa", cq))
            for j in range(4):
                for tb in range(2):
                    bk = nb(allb)
                    pe_begin([slab_b, oa_b[tb]], [bankB[bk]])
                    for k in range(8):
                        ins = nc.tensor.matmul(banks_t[bk][:, :], lhsT=view[:, k, j * 128:(j + 1) * 128],
                                               rhs=oaT[:, k, tb * 512:(tb + 1) * 512], start=(k == 0), stop=(k == 7))
                    pe_end(ins, [slab_b, oa_b[tb]], [bankB[bk]])
                    dst = SGA[:, j, tb * 512:(tb + 1) * 512]
                    op(DVE, [bankB[bk]], [sga_b[(j, tb)]], lambda: nc.vector.tensor_tensor(
                        out=dst, in0=banks_t[bk][:, :], in1=dst, op=ALU.mult))
            ring_release()
            slab_b, view = ring_next(("win", 7168 + cq * 512))
            for j in range(4):
                for tb in range(2):
                    bk = nb(allb)
                    fm_group(bk, slab_b, view, j, hT, hT_b[tb], tb * 512, 512)
                    op(ACT, [bankB[bk]], [sgb_b[(j, tb)]], lambda: nc.scalar.activation(
                        out=SGB[:, j, tb * 512:(tb + 1) * 512], in_=banks_t[bk][:, :], func=AF.Sigmoid))
            ring_release()
            for (m, s) in a4_plan[(cq, 1)]:
                do_ada_slab(m, s)
            if cq == 0:
                finish_mod(2)
                finish_mod(3)
            if cq == 2:
                finish_mod(4)
            slab_b, view = ring_next(("wpb", cq))
            for j in range(4):
                for tb in range(2):
                    bk = nb(allb)
                    pe_begin([slab_b] + ob_tb[tb], [bankB[bk]])
                    for k in range(8):
                        ins = nc.tensor.matmul(banks_t[bk][:, :], lhsT=view[:, k, j * 128:(j + 1) * 128],
                                               rhs=obT[:, k, tb * 512:(tb + 1) * 512], start=(k == 0), stop=(k == 7))
                    pe_end(ins, [slab_b] + ob_tb[tb], [bankB[bk]])
                    i = tmpc[0] % 3
                    tmpc[0] += 1
                    op(DVE, [bankB[bk], sgb_b[(j, tb)]], [tmp_b[i]], lambda: nc.vector.tensor_tensor(
                        out=tmp_v[i], in0=banks_t[bk][:, :], in1=SGB[:, j, tb * 512:(tb + 1) * 512], op=ALU.mult))
                    op(DVE, [tmp_b[i], sga_b[(j, tb)]], [m_b[tb]], lambda: nc.vector.tensor_tensor(
                        out=mT[:, cq * 4 + j, tb * 512:(tb + 1) * 512], in0=tmp_v[i],
                        in1=SGA[:, j, tb * 512:(tb + 1) * 512], op=ALU.add))
            ring_release()
        rtile.free()
        sgb_t.free()
        sga_t.free()
        gu_t.free()
        tr.retire(oa_b)
        qT_t.free()
        hT_t.free()

        def build_bcast(Gmod, v, dst, dst_b):
            for cb in range(4):
                i = tmpc[0] % 3
                tmpc[0] += 1
                dg = tmp_v[i].rearrange("p (c q) -> p c q", q=128)
                g4 = Gmod[:, cb * 4:cb * 4 + 4, v:v + 1].to_broadcast([128, 4, 128])
                o4 = onesf.unsqueeze(1).to_broadcast([128, 4, 128])
                op(DVE, [cst_b, A_b], [tmp_b[i]], lambda: nc.vector.tensor_tensor(out=dg, in0=o4, in1=g4, op=ALU.mult))
                bk = nb(allb)
                pe_begin([tmp_b[i], cst_b], [bankB[bk]])
                for cc in range(4):
                    ins = nc.tensor.transpose(out=banks_t[bk][:, cc * 128:(cc + 1) * 128], in_=dg[:, cc, :], identity=identf)
                pe_end(ins, [tmp_b[i], cst_b], [bankB[bk]])
                op(ACT, [bankB[bk]], [dst_b], lambda: nc.scalar.copy(out=dst[:, cb * 512:(cb + 1) * 512], in_=banks_t[bk][:, :]))

        h2T_t = Tile("h2T", 16 * 1024 * 2)
        h2T = h2T_t.bf().rearrange("p (k t) -> p k t", t=1024)
        h2_b = [[h2T_t.buf("_Pe"), h2T_t.buf("_Po")], [h2T_t.buf("_Se"), h2T_t.buf("_So")]]
        Y_t = Tile("Y", 4 * D * 4)
        Y = Y_t.f32().rearrange("p (t d) -> p t d", d=D)
        Y_b = [Y_t.buf(f"_{t}") for t in range(4)]
        Yg_b = Y_t.buf("_g")
        gn_t = Tile("gn", D * 4)
        GN = gn_t.f32()
        gn_b = gn_t.buf()
        xt_t = Tile("xt", 2 * D * 4)
        xt_b = [xt_t.buf("_0"), xt_t.buf("_1")]
        xt_v = [xt_t.f32(D, i * D * 4) for i in range(2)]
        x1_d = [Buf(f"x1d{t}") for t in range(8)]
        B2 = MODS[:, 3]
        for half in range(2):
            build_bcast(G1, half, GN, gn_b)
            ssy = stat_cols(16).rearrange("p (t c) -> p t c", c=4)
            ssb = tr.buf("ssy")

            xsrc = []
            for t in range(4):
                gt = half * 4 + t
                if t < 2:
                    xv, xb = xt_v[t].rearrange("p (c f) -> p c f", f=256), xt_b[t]
                else:
                    c0 = (t - 2) * 8
                    xv, xb = h2T[:, c0:c0 + 8, half * 512:(half + 1) * 512].bitcast(F32), h2_b[half][t - 2]
                dma(SP, xv, xin[gt * 128:(gt + 1) * 128, :].rearrange("p (c f) -> p c f", f=256), [], h2_b[half] if t >= 2 else [xb], xb)
                xsrc.append((xv, xb))
            for cb in range(4):
                slab_b, view = ring_next(("wo", half, cb))
                for t in range(4):
                    bk = nb(allb)
                    tm_group(bk, slab_b, view, mT, [m_b[half]], half * 512 + t * 128)
                    jv, jb = jq()
                    op(ACT, [bankB[bk]], [ssb, jb], lambda: nc.scalar.activation(
                        out=jv, in_=banks_t[bk][:, :], func=AF.Square, accum_out=ssy[:, t, cb:cb + 1]))
                    op(DVE, [bankB[bk], gn_b], [Y_b[t]], lambda: nc.vector.tensor_tensor(
                        out=Y[:, t, cb * 512:(cb + 1) * 512], in0=banks_t[bk][:, :], in1=GN[:, cb * 512:(cb + 1) * 512], op=ALU.mult))
                ring_release()
            rsy = stat_cols(4)
            op(DVE, [ssb], [ssb], lambda: nc.vector.tensor_reduce(out=rsy, in_=ssy, axis=mybir.AxisListType.X, op=ALU.add))
            op(DVE, [ssb], [ssb], lambda: nc.vector.tensor_scalar(
                out=rsy, in0=rsy, scalar1=1.0 / D, scalar2=EPS, op0=ALU.mult, op1=ALU.add))
            op(ACT, [ssb], [ssb], lambda: nc.scalar.activation(out=rsy, in_=rsy, func=AF.Sqrt))
            op(DVE, [ssb], [ssb], lambda: nc.vector.reciprocal(out=rsy, in_=rsy))
            ss2h = stat_cols(4)
            rs2h = stat_cols(4)
            s2b = tr.buf("ss2h")
            for t in range(4):
                gt = half * 4 + t
                xv, xb = xsrc[t]
                yv = Y[:, t, :].rearrange("p (c f) -> p c f", f=256)
                op(DVE, [ssb] + (h2_b[half] if t >= 2 else [xb]), [Y_b[t]], lambda: nc.vector.scalar_tensor_tensor(
                    out=yv, in0=yv, scalar=rsy[:, t:t + 1], in1=xv, op0=ALU.mult, op1=ALU.add))
                dma(ACT, x1s[gt * 128:(gt + 1) * 128, :], Y[:, t, :], [Y_b[t]], [x1_d[gt]], Y_b[t])
                op(ACT, [Y_b[t]], [s2b] + junk_b, lambda: nc.scalar.activation(
                    out=junk, in_=Y[:, t, :], func=AF.Square, accum_out=ss2h[:, t:t + 1]))
            op(DVE, [s2b], [s2b], lambda: nc.vector.tensor_scalar(
                out=rs2h, in0=ss2h, scalar1=1.0 / D, scalar2=EPS, op0=ALU.mult, op1=ALU.add))
            op(ACT, [s2b], [s2b], lambda: nc.scalar.activation(out=rs2h, in_=rs2h, func=AF.Sqrt))
            op(DVE, [s2b], [s2b], lambda: nc.vector.reciprocal(out=rs2h, in_=rs2h))
            dg = tmp_t.f32(512).rearrange("p (t q) -> p t q", q=128)
            for t in range(4):
                op(DVE, [s2b, cst_b], [tmp_b[0]], lambda: nc.vector.tensor_scalar(
                    out=dg[:, t, :], in0=identf, scalar1=rs2h[:, t:t + 1], scalar2=None, op0=ALU.mult))
            Yg_b.w, Yg_b.r = {}, {}
            for t in range(4):
                _merge(Yg_b.w, Y_b[t].w)
                _merge(Yg_b.r, Y_b[t].r)
            transpose_mod(Y, Yg_b, 4, A2, B2, half, h2T, h2_b[half], half * 512, allb, diag=dg, diag_b=tmp_b[0])
            for t in range(4):
                Y_b[t].w = dict(Yg_b.w)
                Y_b[t].r = dict(Yg_b.r)
        xt_t.free()
        gn_t.free()
        Y_t.free()
        mT_t.free()

        act_t = Tile("actT", 44 * 1024 * 2)
        actT = act_t.bf().rearrange("p (c t) -> p c t", t=1024)
        act_b = [act_t.buf("_P"), act_t.buf("_S")]
        sg_b, sg_v, sgc = tmp_b, tmp_v, tmpc
        for cbp in range(22):
            gu_sb, (gview, uview) = ring_next(("wgu", cbp))
            for j in range(2):
                for tb in range(2):
                    bg = nb(allb)
                    fm_group(bg, gu_sb, gview, j, h2T, h2_b[tb], tb * 512, 512)
                    bu = nb(allb)
                    fm_group(bu, gu_sb, uview, j, h2T, h2_b[tb], tb * 512, 512)
                    i = sgc[0] % 3
                    sgc[0] += 1
                    op(ACT, [bankB[bg]], [sg_b[i]], lambda: nc.scalar.activation(out=sg_v[i], in_=banks_t[bg][:, :], func=AF.Silu))
                    op(DVE, [bankB[bu], sg_b[i]], [act_b[tb]], lambda: nc.vector.tensor_tensor(
                        out=actT[:, cbp * 2 + j, tb * 512:(tb + 1) * 512], in0=banks_t[bu][:, :], in1=sg_v[i], op=ALU.mult))
            ring_release()
        h2T_t.free()

        gn2_t = Tile("gn2", 2 * D * 4)
        gn2_b = [gn2_t.buf("_0"), gn2_t.buf("_1")]
        gn2_v = [gn2_t.f32(D, i * D * 4) for i in range(2)]
        build_bcast(G2, 0, gn2_v[0], gn2_b[0])
        build_bcast(G2, 1, gn2_v[1], gn2_b[1])
        ys_t = Tile("ystg", 4 * 512 * 4)
        ys_b = [ys_t.buf(f"_{i}") for i in range(4)]
        ys_v = [ys_t.f32(512, i * 2048) for i in range(4)]
        ysc = [0]
        ss2 = stat_cols(32).rearrange("p (t c) -> p t c", c=4)
        rs2 = stat_cols(8)
        ss2b = [tr.buf(f"ss2_{t}") for t in range(8)]
        y2_d = [Buf(f"y2d{t}") for t in range(8)]
        fxs = [None] * 8
        z012 = [None] * 8
        z3 = [None] * 8
        act_inh = {}

        def carve_slot(n_slab, tiles):
            i = n_slab % NSLOT
            tr.retire([slot_bufs[i]])
            base = ring_t.off + i * SLOT_ELEMS * 2
            for j, t in enumerate(tiles):
                zb = tr.buf(f"z012_{t}")
                zv = sb.f32(base + j * 8192, 1536)
                dma(SP, zv, y2s[t * 128:(t + 1) * 128, 0:1536], [y2_d[t]], [zb], zb)
                z012[t] = (zv, zb)
                z3[t] = (sb.f32(base + j * 8192 + 6144, 512), tr.buf(f"z3_{t}"))

        def prefetch_x1(tiles):
            inh = {}
            for ab in act_b:
                _merge(inh, ab.w)
                _merge(inh, ab.r)
            for t in tiles:
                fb = Buf(f"fx{t}", inherit=inh)
                fv = sb.f32(act_t.off + t * D * 4, D)
                dma(ACT, fv, x1s[t * 128:(t + 1) * 128, :], [x1_d[t]], [fb], fb)
                fxs[t] = (fv, fb)

        def final_rs_a(t):
            sbt = ss2b[t]
            r = rs2[:, t:t + 1]
            op(DVE, [sbt], [sbt], lambda: nc.vector.tensor_reduce(out=r, in_=ss2[:, t, :], axis=mybir.AxisListType.X, op=ALU.add))
            op(DVE, [sbt], [sbt], lambda: nc.vector.tensor_scalar(out=r, in0=r, scalar1=1.0 / D, scalar2=EPS, op0=ALU.mult, op1=ALU.add))
            op(ACT, [sbt], [sbt], lambda: nc.scalar.activation(out=r, in_=r, func=AF.Sqrt))

        def final_tile(t):
            fx, fxb = fxs[t]
            zv, zb = z012[t]
            z3v, z3b = z3[t]
            sbt = ss2b[t]
            r = rs2[:, t:t + 1]
            op(DVE, [sbt], [sbt], lambda: nc.vector.reciprocal(out=r, in_=r))
            zbs = list(zb) if isinstance(zb, list) else [zb]
            op(DVE, [sbt] + zbs, [fxb], lambda: nc.vector.scalar_tensor_tensor(
                out=fx[:, 0:1536], in0=zv, scalar=r, in1=fx[:, 0:1536], op0=ALU.mult, op1=ALU.add))
            op(DVE, [sbt, z3b], [fxb], lambda: nc.vector.scalar_tensor_tensor(
                out=fx[:, 1536:2048], in0=z3v, scalar=r, in1=fx[:, 1536:2048], op0=ALU.mult, op1=ALU.add))
            dst = yp_d[t * 128:(t + 1) * 128, :] if t < 4 else ys_d[(t - 4) * 128:(t - 3) * 128, :]
            dma(POOL, dst, fx, [fxb], [], fxb, is_output=True)

        for cb in range(4):
            for kk, (r0, kc) in enumerate(KK):
                slab_b, view = ring_next(("wd", cb, kk))
                n_this = ring["consumed"] - 1
                last = (cb == 3 and kk == 2)
                for t in range(8):
                    rd = [slab_b, act_b[t // 4]]
                    pe_begin(rd, [bankB[t]])
                    for k in range(kc):
                        ins = nc.tensor.matmul(banks_t[t][:, :], lhsT=actT[:, r0 + k, t * 128:(t + 1) * 128],
                                               rhs=view[:, k, :], start=(kk == 0 and k == 0), stop=(kk == 2 and k == kc - 1))
                    pe_end(ins, rd, [bankB[t]])
                    if last and t == 7:
                        inh = {}
                        for ab in act_b:
                            _merge(inh, ab.w)
                            _merge(inh, ab.r)
                        zb7 = Buf("z012_7", inherit=inh)
                        zv7 = sb.f32(act_t.off + 32 * 2048, 1536)
                        dma(SP, zv7, y2s[7 * 128:8 * 128, 0:1536], [y2_d[7]], [zb7], zb7)
                        z012[7] = (zv7, zb7)
                    if kk == 2:
                        if last and t >= 1:
                            final_rs_a(t - 1)
                        jv, jb = jq()
                        op(ACT, [bankB[t]], [ss2b[t], jb], lambda: nc.scalar.activation(
                            out=jv, in_=banks_t[t][:, :], func=AF.Square, accum_out=ss2[:, t, cb:cb + 1]))
                        if not last:
                            i = ysc[0] % 4
                            ysc[0] += 1
                            op(DVE, [bankB[t], gn2_b[t // 4]], [ys_b[i]], lambda: nc.vector.tensor_tensor(
                                out=ys_v[i], in0=banks_t[t][:, :], in1=gn2_v[t // 4][:, cb * 512:(cb + 1) * 512], op=ALU.mult))
                            dma(SP, y2s[t * 128:(t + 1) * 128, cb * 512:(cb + 1) * 512], ys_v[i], [ys_b[i]], [y2_d[t]], ys_b[i])
                        else:
                            if t >= 4:
                                z3[t] = (ys_v[t - 4], ys_b[t - 4])
                            z3v, z3b = z3[t]
                            op(DVE, [bankB[t], gn2_b[t // 4]], [z3b], lambda: nc.vector.tensor_tensor(
                                out=z3v, in0=banks_t[t][:, :], in1=gn2_v[t // 4][:, cb * 512:(cb + 1) * 512], op=ALU.mult))
                            if t >= 1:
                                final_tile(t - 1)
                ring_release()
                if cb == 3 and kk == 0:
                    prefetch_x1([0, 1, 2, 3])
                    carve_slot(n_this, [0, 1])
                if cb == 3 and kk == 1:
                    carve_slot(n_this, [2, 3])
                    prefetch_x1([4, 5, 6, 7])
                    for t, v in ((4, 0), (5, 1)):
                        inh = {}
                        _merge(inh, gn2_b[v].w)
                        _merge(inh, gn2_b[v].r)
                        zb = Buf(f"z012_{t}", inherit=inh)
                        zv = gn2_v[v][:, 0:1536]
                        dma(SP, zv, y2s[t * 128:(t + 1) * 128, 0:1536], [y2_d[t]], [zb], zb)
                        z012[t] = (zv, zb)
                    zv = tmp_t.f32(1536)
                    dma(SP, zv, y2s[6 * 128:7 * 128, 0:1536], [y2_d[6]], tmp_b, tmp_b[0])
                    z012[6] = (zv, tmp_b)
        final_rs_a(7)
        final_tile(7)

        for k, (s, v) in final_events.items():
            SP.wait_ev(s, v)
        assert ring["consumed"] == len(ring["sched"]) == ring["released"], (ring["consumed"], len(ring["sched"]), ring["released"])
    return nc


_PROG = {}


def _bias_table(rpb, half):
    H = rpb.shape[0]
    lr = np.arange(12)
    gr = np.where(lr < 8, lr + 8 * half, (lr if half == 0 else lr - 4))
    qi = np.arange(8)
    r = qi + 8 * half
    rs = np.clip(r - 4, 0, 8)
    row_ok = (gr[None, :] >= rs[:, None]) & (gr[None, :] < rs[:, None] + 8)
    if half == 0:
        row_ok[:, 11] = row_ok[:, 11]
    dr = gr[None, :] - r[:, None] + 7
    cols = np.arange(64)
    cs = np.clip(cols - 8, 0, 48)
    col_ok = (cols[None, :] >= cs[:, None]) & (cols[None, :] < cs[:, None] + 16)
    dc = np.clip(cols[None, :] - cols[:, None] + 15, 0, 30)
    drc = np.clip(dr, 0, 14)
    g = rpb[:, drc[:, :, None, None], dc[None, None, :, :]]
    ok = row_ok[:, :, None, None] & col_ok[None, None, :, :]
    g = np.where(ok[None], g, np.float32(NEG)).astype(np.float32)
    g = g.transpose(0, 2, 4, 1, 3).reshape(H, 12 * 64, 512)
    g = g.reshape(H, 6, 128, 512).transpose(0, 2, 1, 3)
    return np.ascontiguousarray(g)


def kernel(x_prompt, x_sample, cache_k, cache_v, c, c_ctx, w_ada, b_ada,
           norm_mix_pre, norm_mix_post, norm_ffn_pre, norm_ffn_post, w_in, rpb,
           ln_v, w_s, b_s, w_pa, w_pb, w_o, w_gate, w_up, w_down):
    f = lambda a: np.ascontiguousarray(np.asarray(a, dtype=np.float32))
    x_prompt, x_sample, cache_k, cache_v = f(x_prompt), f(x_sample), f(cache_k), f(cache_v)
    c, c_ctx = f(c), f(c_ctx)
    if "nc" not in _PROG:
        _PROG["nc"] = build_program()
    nc = _PROG["nc"]
    shared = {
        "w_ada": f(w_ada)[0], "w_in": f(w_in)[0], "w_pa": f(w_pa)[0], "w_pb": f(w_pb)[0], "w_o": f(w_o)[0],
        "w_gate": f(w_gate)[0], "w_up": f(w_up)[0], "w_down": f(w_down)[0],
        "b_adaT": np.ascontiguousarray(f(b_ada)[0].reshape(6, 16, 128).transpose(2, 0, 1)),
        "nvT": np.ascontiguousarray(np.stack([f(norm_mix_pre)[0], f(norm_mix_post)[0], f(norm_ffn_pre)[0],
                                              f(norm_ffn_post)[0]]).reshape(4, 16, 128).transpose(2, 0, 1)),
        "ln_v": f(ln_v)[0].reshape(1, 1024),
        "w_sT": np.ascontiguousarray(f(w_s)[0].transpose(2, 0, 1)),
        "b_s": f(b_s)[0].reshape(1, 1024),
    }
    bias_tabs = [_bias_table(f(rpb)[0], 0), _bias_table(f(rpb)[0], 1)]
    in_maps = []
    for i in range(8):
        b, half = i // 2, i % 2
        xs = x_sample[b]
        own = xs[half * 512:(half + 1) * 512]
        halo = xs[512:768] if half == 0 else xs[256:512]
        xin = np.concatenate([x_prompt[2 * i:2 * i + 2].reshape(512, D), own, halo], axis=0)
        cv2 = np.stack([c_ctx, c[b]], axis=-1)
        m = dict(shared)
        m.update({
            "xin": np.ascontiguousarray(xin),
            "ck": np.ascontiguousarray(cache_k[b, 0].reshape(512, 1024)),
            "cv": np.ascontiguousarray(cache_v[b, 0].reshape(512, 1024)),
            "cvT": np.ascontiguousarray(cv2.reshape(16, 128, 2).transpose(1, 0, 2)),
            "biasT": bias_tabs[half],
        })
        in_maps.append(m)
    res = run_bass_kernel_spmd(nc, in_maps, core_ids=list(range(8)))
    R = res.results
    y_prompt = np.stack([R[i]["yp"].reshape(2, 256, D) for i in range(8)]).reshape(16, 256, D)
    y_sample = np.stack([R[i]["ys"] for i in range(8)]).reshape(4, 1024, D)
    state_k = np.stack([R[i]["sk"].reshape(2, 1, 256, NH, DH) for i in range(8)]).reshape(16, 1, 256, NH, DH)
    state_v = np.stack([R[i]["sv"].reshape(2, 1, 256, NH, DH) for i in range(8)]).reshape(16, 1, 256, NH, DH)
    return (y_prompt.astype(np.float32), y_sample.astype(np.float32),
            state_k.astype(np.float32), state_v.astype(np.float32))
```

```python
import contextlib
import numpy as np
import concourse.bass as bass
import concourse.mybir as mybir
from concourse.bass_utils import run_bass_kernel_spmd

F32 = mybir.dt.float32
BF16 = mybir.dt.bfloat16
AF = mybir.ActivationFunctionType
ALU = mybir.AluOpType

D = 2048
NH = 8
DH = 128
DFF = 5632
DIN = 9216
NKC = 16
EPS = 1e-6
ATTN_SCALE = DH ** -0.5
NEG = -30000.0
NSLOT = 3
SLOT_ELEMS = 8192
SB_BYTES = 207 * 1024
DEBUG = False


class Buf:
    __slots__ = ("name", "w", "r", "excl", "dsem", "dcnt", "ssem", "scnt")

    def __init__(self, name, excl=False, inherit=None):
        self.name = name
        self.w = {}
        self.r = dict(inherit) if inherit else {}
        self.excl = excl
        self.dsem = None
        self.dcnt = 0
        self.ssem = None
        self.scnt = 0


def _merge(d, ev):
    for k, (s, v) in ev.items():
        cur = d.get(k)
        if cur is None or cur[1] < v:
            d[k] = (s, v)


class Engine:
    def __init__(self, tr, name, e, own_sem=True):
        self.tr = tr
        self.name = name
        self.e = e
        self.sem = tr.new_sem("s_" + name) if own_sem else None
        self.cnt = 0
        self.waited = {}

    def wait_ev(self, s, v):
        k = s.num
        if self.waited.get(k, 0) >= v:
            return
        self.e.wait_ge(s, v)
        self.waited[k] = v

    def wait_deps(self, reads, writes, skip_own=False):
        d = {}
        for b in reads:
            _merge(d, b.w)
            if b.excl:
                _merge(d, b.r)
        for b in writes:
            _merge(d, b.w)
            _merge(d, b.r)
        for k, (s, v) in d.items():
            if skip_own and self.sem is not None and s.num == self.sem.num:
                continue
            self.wait_ev(s, v)

    def signal(self, ins):
        self.cnt += 1
        ins.then_inc(self.sem, 1)
        return (self.sem, self.cnt)


def _commit(ev, reads, writes):
    s, v = ev
    k = s.num
    wset = set(id(b) for b in writes)
    for b in reads:
        if id(b) in wset:
            continue
        if b.excl:
            b.w = {k: (s, v)}
            b.r = {}
        else:
            cur = b.r.get(k)
            if cur is None or cur[1] < v:
                b.r[k] = (s, v)
    for b in writes:
        b.w = {k: (s, v)}
        b.r = {}


class Tracker:
    def __init__(self, nc, es):
        self.nc = nc
        self.es = es
        self.nsem = 0
        self.free_events = {}

    def new_sem(self, name):
        self.nsem += 1
        return self.es.enter_context(self.nc.semaphore(name))

    def buf(self, name, excl=False):
        return Buf(name, excl=excl, inherit=self.free_events)

    def retire(self, bufs):
        for b in bufs:
            _merge(self.free_events, b.w)
            _merge(self.free_events, b.r)


class SbAlloc:
    def __init__(self, big, nbytes):
        self.big = big
        self.free = [(0, nbytes)]
        self.live = {}

    def alloc(self, name, nbytes):
        nbytes = (nbytes + 63) // 64 * 64
        for i, (o, n) in enumerate(self.free):
            if n >= nbytes:
                if n == nbytes:
                    self.free.pop(i)
                else:
                    self.free[i] = (o + nbytes, n - nbytes)
                self.live[name] = (o, nbytes)
                return o
        raise RuntimeError(f"SBUF alloc failed for {name} ({nbytes} B); free={self.free}")

    def release(self, name):
        o, n = self.live.pop(name)
        self.free.append((o, n))
        self.free.sort()
        merged = []
        for (a, b) in self.free:
            if merged and merged[-1][0] + merged[-1][1] == a:
                merged[-1] = (merged[-1][0], merged[-1][1] + b)
            else:
                merged.append((a, b))
        self.free = merged

    def bf(self, off, n):
        return self.big[:, off // 2: off // 2 + n]

    def f32(self, off, n):
        return self.big[:, off // 2: off // 2 + 2 * n].bitcast(F32)


def build_program():
    nc = bass.Bass("TRN2", target_bir_lowering=False)

    def din(name, shape):
        return nc.dram_tensor(name, list(shape), F32, kind="ExternalInput").ap()

    def dout(name, shape):
        return nc.dram_tensor(name, list(shape), F32, kind="ExternalOutput").ap()

    xin = din("xin", [1280, D])
    ck_d = din("ck", [512, 1024])
    cv_d = din("cv", [512, 1024])
    cvT_d = din("cvT", [128, 16, 2])
    w_ada = din("w_ada", [D, 6 * D])
    b_adaT_d = din("b_adaT", [128, 6, 16])
    nvT_d = din("nvT", [128, 4, 16])
    w_in = din("w_in", [D, DIN])
    biasT_d = din("biasT", [NH, 128, 6, 512])
    lnv_d = din("ln_v", [1, 1024])
    wsT_d = din("w_sT", [128, 8, 128])
    bs_d = din("b_s", [1, 1024])
    w_pa = din("w_pa", [1024, D])
    w_pb = din("w_pb", [1024, D])
    w_o = din("w_o", [D, D])
    w_gate = din("w_gate", [D, DFF])
    w_up = din("w_up", [D, DFF])
    w_down = din("w_down", [DFF, D])
    yp_d = dout("yp", [512, D])
    ys_d = dout("ys", [512, D])
    sk_d = dout("sk", [512, 1024])
    sv_d = dout("sv", [512, 1024])
    x1s = nc.dram_tensor("x1s", [1024, D], F32).ap()
    y2s = nc.dram_tensor("y2s", [1024, D], F32).ap()
    dbg = {}

    es = contextlib.ExitStack()
    with es:
        tr = Tracker(nc, es)
        big = es.enter_context(nc.sbuf_tensor("big", [128, SB_BYTES // 2], BF16))
        sb = SbAlloc(big, SB_BYTES)
        banks_t = [es.enter_context(nc.psum_tensor(f"bank{i}", [128, 512], F32)) for i in range(8)]
        bankB = [Buf(f"bank{i}", excl=True) for i in range(8)]

        PE = Engine(tr, "pe", nc.tensor)
        ACT = Engine(tr, "act", nc.scalar)
        DVE = Engine(tr, "dve", nc.vector)
        POOL = Engine(tr, "pool", nc.gpsimd)
        SP = Engine(tr, "sp", nc.sync, own_sem=False)
        final_events = {}

        def op(eng, reads, writes, fn):
            eng.wait_deps(reads, writes)
            ins = fn()
            ev = eng.signal(ins)
            _commit(ev, reads, writes)
            return ev

        def dma(q, out, in_, reads, writes, dbuf, is_output=False):
            q.wait_deps(reads, writes)
            ins = q.e.dma_start(out=out, in_=in_)
            if q is POOL:
                if dbuf.ssem is None:
                    dbuf.ssem = tr.new_sem("w_" + dbuf.name)
                dbuf.scnt += 1
                ins.then_inc(dbuf.ssem, 16)
                ev = (dbuf.ssem, 16 * dbuf.scnt)
            else:
                if dbuf.dsem is None:
                    dbuf.dsem = tr.new_sem("d_" + dbuf.name)
                dbuf.dcnt += 1
                ins.then_inc(dbuf.dsem, 16)
                ev = (dbuf.dsem, 16 * dbuf.dcnt)
            _commit(ev, reads, writes)
            if is_output:
                _merge(final_events, {ev[0].num: ev})
            return ev

        def pe_begin(reads, writes):
            PE.wait_deps(reads, writes, skip_own=True)

        def pe_end(ins, reads, writes):
            ev = PE.signal(ins)
            _commit(ev, reads, writes)

        class Tile:
            def __init__(self, name, nbytes):
                self.name = name
                self.off = sb.alloc(name, nbytes)
                self.nbytes = nbytes
                self.bufs = []

            def buf(self, suffix=""):
                b = tr.buf(self.name + suffix)
                self.bufs.append(b)
                return b

            def bf(self, n=None, o=0):
                n = (self.nbytes - o) // 2 if n is None else n
                return sb.bf(self.off + o, n)

            def f32(self, n=None, o=0):
                n = (self.nbytes - o) // 4 if n is None else n
                return sb.f32(self.off + o, n)

            def free(self):
                tr.retire(self.bufs)
                sb.release(self.name)

        cst = Tile("cst", 24 * 1024)
        cst_b = cst.buf()
        co = [0]

        def cf32(n):
            v = cst.f32(n, co[0])
            co[0] += 4 * n
            return v

        def cbf(n):
            v = cst.bf(n, co[0])
            co[0] += 2 * n
            return v

        identf = cf32(128)
        identb = cbf(128)
        onesb = cbf(128)
        onesf = cf32(128)
        sel = cf32(128)
        cvT = cf32(32).rearrange("p (k v) -> p k v", v=2)
        csb = cbf(32).rearrange("p (k v) -> p k v", v=2)
        b_adaT = cf32(96).rearrange("p (m c) -> p m c", c=16)
        nvT = cf32(64).rearrange("p (m c) -> p m c", c=16)
        MODS = cf32(192).rearrange("p (m c v) -> p m c v", m=6, c=16, v=2)
        A1 = cf32(32).rearrange("p (c v) -> p c v", v=2)
        A2 = cf32(32).rearrange("p (c v) -> p c v", v=2)
        G1 = cf32(32).rearrange("p (c v) -> p c v", v=2)
        G2 = cf32(32).rearrange("p (c v) -> p c v", v=2)
        LNG = cf32(1024)
        BS_flat = cf32(1024)
        BS = BS_flat.rearrange("p (g q) -> p g q", q=128)
        wsT = cbf(1024).rearrange("p (g q) -> p g q", q=128)
        stats = cf32(512)
        eps_t = cf32(8)
        assert co[0] <= 24 * 1024, co[0]

        csem = tr.new_sem("d_const")
        ncl = 0
        for (o_, i_) in [(cvT, cvT_d), (b_adaT, b_adaT_d), (nvT, nvT_d)]:
            nc.sync.dma_start(out=o_, in_=i_).then_inc(csem, 16)
            ncl += 1
        nc.sync.dma_start(out=LNG, in_=lnv_d[0, :].partition_broadcast(128)).then_inc(csem, 16)
        ncl += 1
        nc.sync.dma_start(out=BS_flat, in_=bs_d[0, :].partition_broadcast(128)).then_inc(csem, 16)
        ncl += 1
        cload_ev = {csem.num: (csem, 16 * ncl)}

        mods_b = tr.buf("mods")
        A_b = tr.buf("A1A2G")
        stat_b = {}

        scol = [0]

        def stat_cols(n):
            c0 = scol[0]
            scol[0] += n
            assert scol[0] <= 512
            return stats[:, c0:c0 + n]

        junk_t = Tile("junk", 2048 * 2)
        junk = junk_t.bf()
        junk_b = [junk_t.buf(f"_{i}") for i in range(4)]
        jctr = [0]

        def jq():
            i = jctr[0] % 4
            jctr[0] += 1
            return junk[:, i * 512:(i + 1) * 512], junk_b[i]
        tmp_t = Tile("tmp", 3 * 512 * 4)
        tmp_b = [tmp_t.buf(f"_{i}") for i in range(3)]
        tmp_v = [tmp_t.f32(512, i * 2048) for i in range(3)]
        tmpc = [0]
        ring_t = Tile("ring", NSLOT * SLOT_ELEMS * 2)
        slot_bufs = [ring_t.buf(f"_s{i}") for i in range(NSLOT)]
        ring = {"sched": [], "issued": 0, "consumed": 0, "released": 0}

        def slot_view(i):
            return ring_t.bf(SLOT_ELEMS, i * SLOT_ELEMS * 2)

        def ring_issue():
            while ring["issued"] < len(ring["sched"]) and ring["issued"] < ring["released"] + NSLOT:
                n = ring["issued"]
                key, parts = ring["sched"][n]
                i = n % NSLOT
                sbf = slot_bufs[i]
                POOL.wait_deps([], [sbf])
                if sbf.dsem is None:
                    sbf.dsem = tr.new_sem("d_" + sbf.name)
                for (src, eo, kc, ncol) in parts:
                    dst = slot_view(i)[:, eo:eo + kc * ncol].rearrange("p (k n) -> p k n", n=ncol)
                    nc.gpsimd.dma_start(out=dst, in_=src).then_inc(sbf.dsem, 16)
                    sbf.dcnt += 1
                _commit((sbf.dsem, 16 * sbf.dcnt), [], [sbf])
                ring["issued"] += 1

        def ring_next(key):
            n = ring["consumed"]
            k2, parts = ring["sched"][n]
            assert k2 == key, (k2, key)
            ring["consumed"] += 1
            i = n % NSLOT
            views = [slot_view(i)[:, eo:eo + kc * ncol].rearrange("p (k n) -> p k n", n=ncol)
                     for (src, eo, kc, ncol) in parts]
            return slot_bufs[i], (views[0] if len(views) == 1 else views)

        def ring_release():
            ring["released"] += 1
            assert ring["released"] <= ring["consumed"]
            ring_issue()

        def wslab(w, kc, c0, ncol, r0=0):
            return w[r0 * 128:(r0 + kc) * 128, c0:c0 + ncol].rearrange("(k p) n -> p k n", p=128)

        S = ring["sched"]
        ada_order = [(1, s) for s in range(4)] + [(0, s) for s in range(4)]
        ada_rest = [(m, s) for m in (2, 4, 3, 5) for s in range(4)]
        for (m, s) in ada_order:
            S.append((("ada", m, s), [(wslab(w_ada, 16, m * D + s * 512, 512), 0, 16, 512)]))
        rest_iter = list(ada_rest)

        def sched_ada(n):
            grp = []
            for _ in range(n):
                m, s = rest_iter.pop(0)
                S.append((("ada", m, s), [(wslab(w_ada, 16, m * D + s * 512, 512), 0, 16, 512)]))
                grp.append((m, s))
            return grp

        a1_plan = []
        for i, c0 in enumerate(range(0, 3072, 512)):
            S.append((("win", c0), [(wslab(w_in, 16, c0, 512), 0, 16, 512)]))
            a1_plan.append(sched_ada(1))
        for h in range(NH):
            S.append((("bias", h), [(biasT_d[h], 0, 6, 512)]))
        a3_plan = []
        for c0 in (4096, 4608, 3072, 3584):
            S.append((("win", c0), [(wslab(w_in, 16, c0, 512), 0, 16, 512)]))
            a3_plan.append(sched_ada(1))
        a4_plan = {}
        for cq in range(4):
            S.append((("win", 5120 + cq * 512), [(wslab(w_in, 16, 5120 + cq * 512, 512), 0, 16, 512)]))
            a4_plan[(cq, 0)] = sched_ada(1) if cq <= 2 else []
            S.append((("wpa", cq), [(wslab(w_pa, 8, cq * 512, 512), 0, 8, 512)]))
            S.append((("win", 7168 + cq * 512), [(wslab(w_in, 16, 7168 + cq * 512, 512), 0, 16, 512)]))
            a4_plan[(cq, 1)] = sched_ada(1) if cq <= 2 else []
            S.append((("wpb", cq), [(wslab(w_pb, 8, cq * 512, 512), 0, 8, 512)]))
        b_plan = []
        for half in range(2):
            for cb in range(4):
                S.append((("wo", half, cb), [(wslab(w_o, 16, cb * 512, 512), 0, 16, 512)]))
            b_plan.append([])
        for cbp in range(22):
            S.append((("wgu", cbp), [(wslab(w_gate, 16, cbp * 256, 256), 0, 16, 256),
                                      (wslab(w_up, 16, cbp * 256, 256), 4096, 16, 256)]))
        assert not rest_iter
        KK = [(0, 16), (16, 16), (32, 12)]
        for cb in range(4):
            for kk, (r0, kc) in enumerate(KK):
                S.append((("wd", cb, kk), [(wslab(w_down, kc, cb * 512, 512, r0=r0), 0, kc, 512)]))
        ring_issue()
        G = nc.gpsimd
        pb_ = [Buf("pinit0"), Buf("pinit1")]
        op(POOL, [], [pb_[0]], lambda: G.memset(identf, 0.0))
        op(POOL, [pb_[0]], [pb_[0]], lambda: G.affine_select(out=identf, in_=identf, pattern=[[-1, 128]],
                                                            compare_op=ALU.not_equal, fill=1.0, base=0, channel_multiplier=1))
        op(POOL, [], [pb_[1]], lambda: G.memset(sel, 0.0))
        op(POOL, [pb_[1]], [pb_[1]], lambda: G.affine_select(out=sel, in_=sel, pattern=[[-1, 128]],
                                                            compare_op=ALU.not_equal, fill=1.0, base=0, channel_multiplier=1))
        G.memset(onesf, 1.0)
        G.memset(stats, 0.0)
        ins = G.memset(eps_t, EPS)
        pool_ev = POOL.signal(ins)
        cst_b.w = {pool_ev[0].num: pool_ev}
        _merge(cst_b.w, cload_ev)
        op(DVE, [cst_b], [], lambda: nc.vector.tensor_copy(out=identb, in_=identf))
        ev = op(DVE, [cst_b], [], lambda: nc.vector.tensor_copy(out=onesb, in_=onesf))
        _merge(cst_b.w, {ev[0].num: ev})
        ev = op(ACT, [cst_b], [], lambda: nc.scalar.activation(out=csb, in_=cvT, func=AF.Silu))
        _merge(cst_b.w, {ev[0].num: ev})
        cst_b.r = {}
        rtile = Tile("rtile", 2 * 2048)
        r_bufs = [rtile.buf("_0"), rtile.buf("_1")]
        kcT_t = Tile("kcT", NH * 512 * 2)
        kcT_b = [kcT_t.buf(f"_{h}") for h in range(NH)]
        kcT = kcT_t.bf().rearrange("p (h t) -> p h t", t=512)
        vc_t = Tile("vc", 4 * 1024 * 2)
        vc_b = vc_t.buf()
        VC = vc_t.bf().rearrange("p (t f) -> p t f", f=1024)
        hT_t = Tile("hT", 16 * 1024 * 2)
        hT = hT_t.bf().rearrange("p (k t) -> p k t", t=1024)
        hT_b = [[hT_t.buf("_Pe"), hT_t.buf("_Po")], [hT_t.buf("_Se"), hT_t.buf("_So")]]
        hTH_t = Tile("hTH", 16 * 256 * 2)
        hTH = hTH_t.bf().rearrange("p (k t) -> p k t", t=256)
        hTH_b = [hTH_t.buf("_e"), hTH_t.buf("_o")]
        KCa = junk.rearrange("p (t f) -> p t f", f=1024)
        KCb = tmp_t.bf(2048).rearrange("p (t f) -> p t f", f=1024)
        xg_t = [Tile("xg0", 4 * D * 4), Tile("xg1", 4 * D * 4)]
        xg_b = [xg_t[0].buf(), xg_t[1].buf()]
        xg_v = [t_.f32().rearrange("p (t d) -> p t d", d=D) for t_ in xg_t]

        def load_x(i, row0, ntile):
            dma(SP, xg_v[i][:, 0:ntile, :], xin[row0:row0 + ntile * 128, :].rearrange("(t p) d -> p t d", p=128),
                [], [xg_b[i]], xg_b[i])


        ada_ctr = [0]

        def do_ada_slab(m, s):
            n = ada_ctr[0]
            ada_ctr[0] += 1
            sbuf_, view = ring_next(("ada", m, s))
            bk = n % 2
            pe_begin([cst_b, sbuf_], [bankB[bk]])
            for k in range(16):
                ins = nc.tensor.matmul(banks_t[bk][0:2, :], lhsT=csb[:, k, :], rhs=view[:, k, :],
                                       start=(k == 0), stop=(k == 15))
            pe_end(ins, [cst_b, sbuf_], [bankB[bk]])
            ring_release()
            rb = r_bufs[n % 2]
            R = rtile.f32(512, (n % 2) * 2048)[0:2, :]
            op(DVE, [bankB[bk]], [rb], lambda: nc.vector.tensor_copy(out=R, in_=banks_t[bk][0:2, :]))
            ada_flush()
            ada_pending.append((m, s, R, rb))

        ada_pending = []

        def ada_flush():
            while ada_pending:
                m, s, R, rb = ada_pending.pop(0)
                pe_begin([rb, cst_b], [bankB[2]])
                for j in range(4):
                    ins = nc.tensor.matmul(banks_t[2][:, 2 * j:2 * j + 2], lhsT=R[:, j * 128:(j + 1) * 128],
                                           rhs=sel[0:2, 0:2], start=True, stop=True)
                pe_end(ins, [rb, cst_b], [bankB[2]])
                for v in range(2):
                    src = banks_t[2][:, 0:8].rearrange("p (j v) -> p j v", v=2)[:, :, v]
                    op(DVE, [bankB[2], cst_b], [mods_b],
                       lambda: nc.vector.tensor_tensor(out=MODS[:, m, 4 * s:4 * s + 4, v], in0=src,
                                                       in1=b_adaT[:, m, 4 * s:4 * s + 4], op=ALU.add))

        def finish_mod(kind):
            ada_flush()
            for v in range(2):
                if kind == 1:
                    op(DVE, [mods_b, cst_b], [A_b], lambda: nc.vector.scalar_tensor_tensor(
                        out=A1[:, :, v], in0=MODS[:, 1, :, v], scalar=1.0, in1=nvT[:, 0, :], op0=ALU.add, op1=ALU.mult))
                elif kind == 2:
                    op(DVE, [mods_b, cst_b], [A_b], lambda: nc.vector.scalar_tensor_tensor(
                        out=A2[:, :, v], in0=MODS[:, 4, :, v], scalar=1.0, in1=nvT[:, 2, :], op0=ALU.add, op1=ALU.mult))
                elif kind == 3:
                    op(DVE, [mods_b, cst_b], [A_b], lambda: nc.vector.tensor_tensor(
                        out=G1[:, :, v], in0=MODS[:, 2, :, v], in1=nvT[:, 1, :], op=ALU.mult))
                else:
                    op(DVE, [mods_b, cst_b], [A_b], lambda: nc.vector.tensor_tensor(
                        out=G2[:, :, v], in0=MODS[:, 5, :, v], in1=nvT[:, 3, :], op=ALU.mult))

        def norm_stats(xg, xg_b, ntile):
            ss = stat_cols(ntile)
            rs = stat_cols(ntile)
            sb_ = tr.buf("st")
            for t in range(ntile):
                op(ACT, [xg_b], [sb_] + junk_b, lambda: nc.scalar.activation(
                    out=junk, in_=xg[:, t, :], func=AF.Square, accum_out=ss[:, t:t + 1]))
            op(DVE, [sb_], [sb_], lambda: nc.vector.tensor_scalar(
                out=rs, in0=ss, scalar1=1.0 / D, scalar2=EPS, op0=ALU.mult, op1=ALU.add))
            op(ACT, [sb_], [sb_], lambda: nc.scalar.activation(out=rs, in_=rs, func=AF.Sqrt))
            op(DVE, [sb_], [sb_], lambda: nc.vector.reciprocal(out=rs, in_=rs))
            for t in range(ntile):
                op(DVE, [sb_, xg_b], [xg_b], lambda: nc.vector.tensor_scalar(
                    out=xg[:, t, :], in0=xg[:, t, :], scalar1=rs[:, t:t + 1], scalar2=None, op0=ALU.mult))

        def transpose_mod(xg, xg_b, ntile, Amod, Bmod, v, dstT, dst_b, dst_tok0, banks, diag=None, diag_b=None, raw=False):
            n = ntile * 128
            for c in range(NKC):
                bk = banks[c % len(banks)]
                rd = [xg_b, cst_b] + ([diag_b] if diag is not None else [])
                pe_begin(rd, [bankB[bk]])
                for t in range(ntile):
                    if diag is None:
                        ins = nc.tensor.transpose(out=banks_t[bk][:, t * 128:(t + 1) * 128],
                                                  in_=xg[:, t, c * 128:(c + 1) * 128], identity=identf)
                    else:
                        ins = nc.tensor.matmul(banks_t[bk][:, t * 128:(t + 1) * 128], lhsT=xg[:, t, c * 128:(c + 1) * 128],
                                               rhs=diag[:, t, :], start=True, stop=True)
                pe_end(ins, rd, [bankB[bk]])
                dst = dstT[:, c, dst_tok0:dst_tok0 + n]
                src = banks_t[bk][:, 0:n]
                db = dst_b[c % 2]
                if raw:
                    if c % 2 == 0:
                        op(ACT, [bankB[bk]], [db], lambda: nc.scalar.copy(out=dst, in_=src))
                    else:
                        op(DVE, [bankB[bk]], [db], lambda: nc.vector.tensor_copy(out=dst, in_=src))
                elif c % 2 == 0:
                    op(ACT, [bankB[bk], A_b, mods_b], [db], lambda: nc.scalar.activation(
                        out=dst, in_=src, func=AF.Identity, scale=Amod[:, c, v:v + 1], bias=Bmod[:, c, v:v + 1]))
                else:
                    op(DVE, [bankB[bk], A_b, mods_b], [db], lambda: nc.vector.tensor_scalar(
                        out=dst, in0=src, scalar1=Amod[:, c, v:v + 1], scalar2=Bmod[:, c, v:v + 1], op0=ALU.mult, op1=ALU.add))

        def modulate(dstT, dst_b, tok0, n, Amod, Bmod, v):
            for c in range(NKC):
                ap = dstT[:, c, tok0:tok0 + n]
                db = dst_b[c % 2]
                if c % 2 == 0:
                    op(ACT, [A_b, mods_b], [db], lambda: nc.scalar.activation(
                        out=ap, in_=ap, func=AF.Identity, scale=Amod[:, c, v:v + 1], bias=Bmod[:, c, v:v + 1]))
                else:
                    op(DVE, [A_b, mods_b], [db], lambda: nc.vector.tensor_scalar(
                        out=ap, in0=ap, scalar1=Amod[:, c, v:v + 1], scalar2=Bmod[:, c, v:v + 1], op0=ALU.mult, op1=ALU.add))

        def norm_transpose(xg, xg_b, ntile, Amod, Bmod, v, dstT, dst_b, dst_tok0, banks):
            norm_stats(xg, xg_b, ntile)
            transpose_mod(xg, xg_b, ntile, Amod, Bmod, v, dstT, dst_b, dst_tok0, banks)

        def kct_load():
            ckv = ck_d.rearrange("(t p) f -> p t f", p=128)
            dma(POOL, KCa, ckv[:, 0:2, :], [], junk_b, junk_b[0])
            dma(POOL, KCb, ckv[:, 2:4, :], [], tmp_b, tmp_b[0])

        def kct_prep():
            for h in range(NH):
                bk = 3 + (h % 4)
                rd = junk_b + tmp_b + [cst_b]
                pe_begin(rd, [bankB[bk]])
                for t in range(4):
                    src = KCa[:, t, h * 128:(h + 1) * 128] if t < 2 else KCb[:, t - 2, h * 128:(h + 1) * 128]
                    ins = nc.tensor.matmul(banks_t[bk][:, t * 128:(t + 1) * 128], lhsT=src, rhs=identb, start=True, stop=True)
                pe_end(ins, rd, [bankB[bk]])
                if h % 2 == 0:
                    op(ACT, [bankB[bk]], [kcT_b[h]], lambda: nc.scalar.copy(out=kcT[:, h, :], in_=banks_t[bk][:, :]))
                else:
                    op(DVE, [bankB[bk]], [kcT_b[h]], lambda: nc.vector.tensor_copy(out=kcT[:, h, :], in_=banks_t[bk][:, :]))

        B1 = MODS[:, 0]
        a0b = [3, 4, 5, 6, 7]
        load_x(0, 0, 4)
        load_x(1, 512, 4)
        for ai, (m, s) in enumerate(ada_order):
            do_ada_slab(m, s)
            if ai == 0:
                norm_stats(xg_v[0], xg_b[0], 4)
            if ai == 1:
                norm_stats(xg_v[1], xg_b[1], 4)
            if ai == 3:
                transpose_mod(xg_v[0], xg_b[0], 4, None, None, 0, hT, hT_b[0], 0, a0b, raw=True)
                load_x(0, 1024, 2)
            if ai == 4:
                transpose_mod(xg_v[1], xg_b[1], 4, None, None, 1, hT, hT_b[1], 512, a0b, raw=True)
                norm_stats(xg_v[0], xg_b[0], 2)
            if ai == 5:
                transpose_mod(xg_v[0], xg_b[0], 2, None, None, 1, hTH, hTH_b, 0, a0b, raw=True)
        finish_mod(1)
        kct_load()
        dma(POOL, VC, cv_d.rearrange("(t p) f -> p t f", p=128), [], [vc_b], vc_b)
        modulate(hT, hT_b[0], 0, 512, A1, B1, 0)
        modulate(hT, hT_b[1], 512, 512, A1, B1, 1)
        modulate(hTH, hTH_b, 0, 256, A1, B1, 1)
        xg_t[1].free()
        xg_t[0].free()

        qT_t = Tile("qT", NH * 1024 * 2)
        qT = qT_t.bf().rearrange("p (h t) -> p h t", t=1024)
        q_b = {}
        for h in range(NH):
            q_b[(h, 0)] = qT_t.buf(f"_{h}p0")
            q_b[(h, 1)] = qT_t.buf(f"_{h}p1")
            q_b[(h, 2)] = qT_t.buf(f"_{h}s")
        kT_t = Tile("kT", NH * 1280 * 2)
        kT = kT_t.bf().rearrange("p (h t) -> p h t", t=1280)
        k_b = {(h, g): kT_t.buf(f"_{h}{g}") for h in range(NH) for g in range(3)}
        v_t = Tile("v", 10 * 1024 * 2)
        Vt = v_t.bf().rearrange("p (t f) -> p t f", f=1024)
        v_b = [v_t.buf(f"_{t}") for t in range(10)]
        stg_t = Tile("stg", 3 * 512 * 4)
        stg_b = [stg_t.buf(f"_{i}") for i in range(3)]
        stg_v = [stg_t.f32(512, i * 2048) for i in range(3)]
        stg_ctr = [0]
        a1_banks = [3, 4, 5, 6, 7]
        bctr = [0]

        def nb(lst):
            b = lst[bctr[0] % len(lst)]
            bctr[0] += 1
            return b

        evac_ctr = [0]

        def evac_copy(bk, out_ap, out_b, scale=None, ncols=512):
            src = banks_t[bk][:, 0:ncols]
            use_act = (evac_ctr[0] % 2 == 0)
            evac_ctr[0] += 1
            if use_act:
                if scale is None:
                    op(ACT, [bankB[bk]], [out_b], lambda: nc.scalar.copy(out=out_ap, in_=src))
                else:
                    op(ACT, [bankB[bk]], [out_b], lambda: nc.scalar.mul(out=out_ap, in_=src, mul=scale))
            else:
                if scale is None:
                    op(DVE, [bankB[bk]], [out_b], lambda: nc.vector.tensor_copy(out=out_ap, in_=src))
                else:
                    op(DVE, [bankB[bk]], [out_b], lambda: nc.vector.tensor_scalar(
                        out=out_ap, in0=src, scalar1=scale, scalar2=None, op0=ALU.mult))

        def fm_group(bk, slab_b, view, j, rhsT, rhs_b, tok0, ntok, nk=16):
            rd = [slab_b] + list(rhs_b)
            pe_begin(rd, [bankB[bk]])
            for k in range(nk):
                ins = nc.tensor.matmul(banks_t[bk][:, 0:ntok], lhsT=view[:, k, j * 128:(j + 1) * 128],
                                       rhs=rhsT[:, k, tok0:tok0 + ntok], start=(k == 0), stop=(k == nk - 1))
            pe_end(ins, rd, [bankB[bk]])

        def tm_group(bk, slab_b, view, lhsT_all, lhs_b, tok0, nk=16, ncol=512):
            rd = [slab_b] + list(lhs_b)
            pe_begin(rd, [bankB[bk]])
            for k in range(nk):
                ins = nc.tensor.matmul(banks_t[bk][:, 0:ncol], lhsT=lhsT_all[:, k, tok0:tok0 + 128],
                                       rhs=view[:, k, 0:ncol], start=(k == 0), stop=(k == nk - 1))
            pe_end(ins, rd, [bankB[bk]])

        def store_state(bk, dst_d, row0, col0):
            i = stg_ctr[0] % 3
            stg_ctr[0] += 1
            op(DVE, [bankB[bk]], [stg_b[i]], lambda: nc.vector.tensor_copy(out=stg_v[i], in_=banks_t[bk][:, :]))
            dma(SP, dst_d[row0:row0 + 128, col0:col0 + 512], stg_v[i], [stg_b[i]], [], stg_b[i], is_output=True)

        for si, c0 in enumerate(range(0, 3072, 512)):
            slab_b, view = ring_next(("win", c0))
            if c0 < 1024:
                for tb in range(2):
                    for j in range(4):
                        h = (c0 // 512) * 4 + j
                        bk = nb(a1_banks)
                        fm_group(bk, slab_b, view, j, hT, hT_b[tb], tb * 512, 512)
                        if tb == 0:
                            use = [q_b[(h, 0)], q_b[(h, 1)]]
                            src = banks_t[bk][:, :]
                            op(DVE, [bankB[bk]], use, lambda: nc.vector.tensor_scalar(
                                out=qT[:, h, 0:512], in0=src, scalar1=ATTN_SCALE, scalar2=None, op0=ALU.mult))
                        else:
                            evac_copy(bk, qT[:, h, 512:1024], q_b[(h, 2)], scale=ATTN_SCALE)
            elif c0 < 2048:
                cc = c0 - 1024
                for j in range(4):
                    h = (cc // 512) * 4 + j
                    for g, (src_T, src_b, t0, nt) in [(1, (hT, hT_b[1], 512, 512)), (2, (hTH, hTH_b, 0, 256))]:
                        bk = nb(a1_banks)
                        fm_group(bk, slab_b, view, j, src_T, src_b, t0, nt)
                        evac_copy(bk, kT[:, h, g * 512:g * 512 + nt], k_b[(h, g)], ncols=nt)
                kb16 = tmp_t.bf(2048).rearrange("p (t f) -> p t f", f=512)
                for t in range(4):
                    bk = nb(a1_banks)
                    tm_group(bk, slab_b, view, hT, hT_b[0], t * 128)
                    op(ACT, [bankB[bk]], tmp_b, lambda: nc.scalar.copy(out=kb16[:, t, :], in_=banks_t[bk][:, :]))
                    store_state(bk, sk_d, t * 128, cc)
                for j in range(4):
                    h = (cc // 512) * 4 + j
                    bk = nb(a1_banks)
                    pe_begin(tmp_b + [cst_b], [bankB[bk]])
                    for t in range(4):
                        ins = nc.tensor.matmul(banks_t[bk][:, t * 128:(t + 1) * 128], lhsT=kb16[:, t, j * 128:(j + 1) * 128],
                                               rhs=identb, start=True, stop=True)
                    pe_end(ins, tmp_b + [cst_b], [bankB[bk]])
                    evac_copy(bk, kT[:, h, 0:512], k_b[(h, 0)])
            else:
                cc = c0 - 2048
                for t in range(10):
                    bk = nb(a1_banks)
                    if t < 8:
                        tm_group(bk, slab_b, view, hT, hT_b[t // 4], t * 128)
                    else:
                        tm_group(bk, slab_b, view, hTH, hTH_b, (t - 8) * 128)
                    op(ACT, [bankB[bk]], [v_b[t]], lambda: nc.scalar.copy(
                        out=Vt[:, t, cc:cc + 512], in_=banks_t[bk][:, :]))
                    if t < 4:
                        store_state(bk, sv_d, t * 128, cc)
            ring_release()
            for (m, s) in a1_plan[si]:
                do_ada_slab(m, s)
            if si == 0:
                kct_prep()
        hTH_t.free()

        pt_t = Tile("pt", 3 * 512 * 2)
        pt_b = [pt_t.buf(f"_{i}") for i in range(3)]
        pt_v = [pt_t.bf(512, i * 1024) for i in range(3)]
        rc_t = Tile("rc", 2 * 512 * 4)
        rc_b = [rc_t.buf("_0"), rc_t.buf("_1")]
        rc_v = [rc_t.f32(512, i * 2048) for i in range(2)]
        st_banks = [0, 1]
        unit = [0]
        ptc = [0]

        def attn_unit(h, qbuf, q_ap, nq, ktiles, oa_ap):
            u = unit[0]
            unit[0] += 1
            ob = 2 + 2 * (u % 3)
            sb_ = 3 + 2 * (u % 3)
            n = len(ktiles)
            pend = None

            def emit_pv(idx, pi):
                (ka, kb_, va, vb_, ba, bb_) = ktiles[idx]
                pe_begin([pt_b[pi], vb_, cst_b], [bankB[ob], bankB[sb_]])
                nc.tensor.matmul(banks_t[ob][:, 0:nq], lhsT=va, rhs=pt_v[pi][:, 0:nq], start=(idx == 0), stop=(idx == n - 1))
                ins = nc.tensor.matmul(banks_t[sb_][:, 0:nq], lhsT=onesb, rhs=pt_v[pi][:, 0:nq], start=(idx == 0),
                                       stop=(idx == n - 1))
                pe_end(ins, [pt_b[pi], vb_, cst_b], [bankB[ob], bankB[sb_]])

            for idx in range(n):
                (ka, kb_, va, vb_, ba, bb_) = ktiles[idx]
                bk = nb(st_banks)
                rd = [kb_, qbuf, cst_b] + ([bb_] if ba is not None else [])
                pe_begin(rd, [bankB[bk]])
                ins = nc.tensor.matmul(banks_t[bk][:, 0:nq], lhsT=ka, rhs=q_ap, start=True, stop=(ba is None))
                if ba is not None:
                    ins = nc.tensor.matmul(banks_t[bk][:, 0:nq], lhsT=identb, rhs=ba, start=False, stop=True)
                pe_end(ins, rd, [bankB[bk]])
                pi = ptc[0] % 3
                ptc[0] += 1
                op(ACT, [bankB[bk]], [pt_b[pi]], lambda: nc.scalar.activation(
                    out=pt_v[pi][:, 0:nq], in_=banks_t[bk][:, 0:nq], func=AF.Exp))
                if pend is not None:
                    emit_pv(*pend)
                pend = (idx, pi)
            emit_pv(*pend)
            ri = u % 2
            op(DVE, [bankB[sb_]], [rc_b[ri]], lambda: nc.vector.reciprocal(out=rc_v[ri][:, 0:nq], in_=banks_t[sb_][:, 0:nq]))
            op(DVE, [bankB[ob], rc_b[ri]], [qbuf], lambda: nc.vector.tensor_tensor(
                out=oa_ap, in0=banks_t[ob][:, 0:nq], in1=rc_v[ri][:, 0:nq], op=ALU.mult))

        def ctx_unit(s_, h):
            kts = []
            for kt in range(2):
                tk0 = s_ * 256 + kt * 128
                kts.append((kT[:, h, tk0:tk0 + 128], k_b[(h, 0)], Vt[:, s_ * 2 + kt, h * 128:(h + 1) * 128],
                            v_b[s_ * 2 + kt], None, None))
            attn_unit(h, q_b[(h, s_)], qT[:, h, s_ * 256:(s_ + 1) * 256], 256, kts, qT[:, h, s_ * 256:(s_ + 1) * 256])

        for h in range(NH):
            bias_b, bview = ring_next(("bias", h))
            kts = []
            for j in range(6):
                g = 1 if j < 4 else 2
                tk0 = 512 + j * 128
                kts.append((kT[:, h, tk0:tk0 + 128], k_b[(h, g)], Vt[:, 4 + j, h * 128:(h + 1) * 128], v_b[4 + j],
                            bview[:, j, :], bias_b))
            for j in range(4):
                kts.append((kcT[:, h, j * 128:(j + 1) * 128], kcT_b[h], VC[:, j, h * 128:(h + 1) * 128], vc_b, None, None))
            attn_unit(h, q_b[(h, 2)], qT[:, h, 512:1024], 512, kts, qT[:, h, 512:1024])
            ring_release()
            ctx_unit(0, h)
            ctx_unit(1, h)
        oaT = qT
        oa_b = [tr.buf("oaP"), tr.buf("oaS")]
        for h in range(NH):
            for g_ in range(3):
                _merge(oa_b[0 if g_ < 2 else 1].w, q_b[(h, g_)].w)
        rc_t.free()
        pt_t.free()
        stg_t.free()
        v_t.free()
        kT_t.free()
        vc_t.free()
        kcT_t.free()

        gv_t = Tile("gv", 8 * 1024 * 2)
        GV = gv_t.bf().rearrange("p (t f) -> p t f", f=1024)
        gv_b = [gv_t.buf(f"_{t}") for t in range(8)]
        gu_t = Tile("gu", 8 * 1024 * 2)
        guT = gu_t.bf().rearrange("p (c t) -> p c t", t=1024)
        gu_b = {(c, tb): gu_t.buf(f"_{c}{tb}") for c in range(8) for tb in range(2)}
        allb = list(range(8))
        s1 = stat_cols(16).rearrange("p (t s) -> p t s", s=2)
        s2 = stat_cols(16).rearrange("p (t s) -> p t s", s=2)
        lst = stat_cols(40).rearrange("p (a t) -> p a t", t=8)
        ln_b = tr.buf("lnstat")
        ws_b = tr.buf("wsT")
        dma(POOL, wsT, wsT_d, [], [ws_b], ws_b)
        for si, c0 in enumerate((4096, 4608)):
            slab_b, view = ring_next(("win", c0))
            cc = c0 - 4096
            for t in range(8):
                bk = nb(allb)
                tm_group(bk, slab_b, view, hT, hT_b[t // 4], t * 128)
                op(ACT, [bankB[bk]], [gv_b[t], ln_b], lambda: nc.scalar.activation(
                    out=GV[:, t, cc:cc + 512], in_=banks_t[bk][:, :], func=AF.Gelu_apprx_tanh,
                    accum_out=s1[:, t, si:si + 1]))
                i = tmpc[0] % 3
                tmpc[0] += 1
                op(DVE, [gv_b[t]], [tmp_b[i], ln_b], lambda: nc.vector.tensor_tensor(
                    out=tmp_v[i], in0=GV[:, t, cc:cc + 512], in1=GV[:, t, cc:cc + 512], op=ALU.mult))
                jv, jb = jq()
                op(ACT, [tmp_b[i]], [ln_b, jb], lambda: nc.scalar.activation(
                    out=jv, in_=tmp_v[i], func=AF.Identity, accum_out=s2[:, t, si:si + 1]))
            ring_release()
            for (m, s) in a3_plan[si]:
                do_ada_slab(m, s)
        mean, ex2, var, rstd_ln = lst[:, 0, :], lst[:, 1, :], lst[:, 2, :], lst[:, 3, :]
        op(DVE, [ln_b], [ln_b], lambda: nc.vector.tensor_tensor(out=mean, in0=s1[:, :, 0], in1=s1[:, :, 1], op=ALU.add))
        op(DVE, [ln_b], [ln_b], lambda: nc.vector.tensor_tensor(out=ex2, in0=s2[:, :, 0], in1=s2[:, :, 1], op=ALU.add))
        op(DVE, [ln_b], [ln_b], lambda: nc.vector.tensor_scalar(out=mean, in0=mean, scalar1=1.0 / 1024, scalar2=None, op0=ALU.mult))
        op(DVE, [ln_b], [ln_b], lambda: nc.vector.tensor_tensor(out=var, in0=mean, in1=mean, op=ALU.mult))
        op(DVE, [ln_b], [ln_b], lambda: nc.vector.scalar_tensor_tensor(
            out=var, in0=ex2, scalar=1.0 / 1024, in1=var, op0=ALU.mult, op1=ALU.subtract))
        op(DVE, [ln_b], [ln_b], lambda: nc.vector.tensor_scalar(out=var, in0=var, scalar1=EPS, scalar2=None, op0=ALU.add))
        op(ACT, [ln_b], [ln_b], lambda: nc.scalar.activation(out=rstd_ln, in_=var, func=AF.Sqrt))
        op(DVE, [ln_b], [ln_b], lambda: nc.vector.reciprocal(out=rstd_ln, in_=rstd_ln))
        for t in range(8):
            for hf in range(2):
                i = tmpc[0] % 3
                tmpc[0] += 1
                op(DVE, [gv_b[t], ln_b], [tmp_b[i]], lambda: nc.vector.tensor_scalar(
                    out=tmp_v[i], in0=GV[:, t, hf * 512:(hf + 1) * 512], scalar1=mean[:, t:t + 1],
                    scalar2=rstd_ln[:, t:t + 1], op0=ALU.subtract, op1=ALU.mult))
                op(DVE, [tmp_b[i], cst_b], [gv_b[t]], lambda: nc.vector.tensor_tensor(
                    out=GV[:, t, hf * 512:(hf + 1) * 512], in0=tmp_v[i], in1=LNG[:, hf * 512:(hf + 1) * 512], op=ALU.mult))
        def gate(gh):
            for t in range(8):
                bk = nb(allb)
                pe_begin([gv_b[t], ws_b], [bankB[bk]])
                for gg in range(4):
                    g = gh * 4 + gg
                    ins = nc.tensor.matmul(banks_t[bk][:, gg * 128:(gg + 1) * 128], lhsT=GV[:, t, g * 128:(g + 1) * 128],
                                           rhs=wsT[:, g, :], start=True, stop=True)
                pe_end(ins, [gv_b[t], ws_b], [bankB[bk]])
                i = tmpc[0] % 3
                tmpc[0] += 1
                tv = tmp_v[i].rearrange("p (g q) -> p g q", q=128)
                op(DVE, [bankB[bk], cst_b], [tmp_b[i]], lambda: nc.vector.tensor_tensor(
                    out=tv, in0=banks_t[bk][:, :].rearrange("p (g q) -> p g q", q=128), in1=BS[:, gh * 4:gh * 4 + 4, :], op=ALU.add))
                gbs = [gu_b[(gh * 4 + gg, t // 4)] for gg in range(4)]
                dst = guT[:, gh * 4:gh * 4 + 4, t * 128:(t + 1) * 128]
                op(DVE, [tmp_b[i]], gbs, lambda: nc.vector.tensor_tensor(out=dst, in0=tv, in1=dst, op=ALU.mult))

        for si, c0 in enumerate((3072, 3584)):
            slab_b, view = ring_next(("win", c0))
            for j in range(4):
                c = si * 4 + j
                for tb in range(2):
                    bk = nb(allb)
                    fm_group(bk, slab_b, view, j, hT, hT_b[tb], tb * 512, 512)
                    op(ACT, [bankB[bk]], [gu_b[(c, tb)]], lambda: nc.scalar.activation(
                        out=guT[:, c, tb * 512:(tb + 1) * 512], in_=banks_t[bk][:, :], func=AF.Gelu_apprx_tanh))
            ring_release()
            for (m, s) in a3_plan[2 + si]:
                do_ada_slab(m, s)
            gate(si)
        obT = guT
        gv_t.free()

        mT_t = Tile("mT", 16 * 1024 * 2)
        mT = mT_t.bf().rearrange("p (c t) -> p c t", t=1024)
        m_b = [mT_t.buf("_P"), mT_t.buf("_S")]
        sga_t = Tile("sga", 4 * 1024 * 2)
        SGA = sga_t.bf().rearrange("p (c t) -> p c t", t=1024)
        sga_b = {(j, tb): sga_t.buf(f"_{j}{tb}") for j in range(4) for tb in range(2)}
        sgb_t = Tile("sgb", 4 * 1024 * 2)
        SGB = sgb_t.bf().rearrange("p (c t) -> p c t", t=1024)
        sgb_b = {(j, tb): sgb_t.buf(f"_{j}{tb}") for j in range(4) for tb in range(2)}
        ob_all = [gu_b[(c, tb)] for c in range(8) for tb in range(2)]
        ob_tb = [[gu_b[(c, tb)] for c in range(8)] for tb in range(2)]
        for cq in range(4):
            slab_b, view = ring_next(("win", 5120 + cq * 512))
            for j in range(4):
                for tb in range(2):
                    bk = nb(allb)
                    fm_group(bk, slab_b, view, j, hT, hT_b[tb], tb * 512, 512)
                    op(ACT, [bankB[bk]], [sga_b[(j, tb)]], lambda: nc.scalar.activation(
                        out=SGA[:, j, tb * 512:(tb + 1) * 512], in_=banks_t[bk][:, :], func=AF.Sigmoid))
            ring_release()
            for (m, s) in a4_plan[(cq, 0)]:
                do_ada_slab(m, s)
            slab_b, view = ring_next(("wpa", cq))
            for j in range(4):
                for tb in range(2):
                    bk = nb(allb)
                    pe_begin([slab_b, oa_b[tb]], [bankB[bk]])
                    for k in range(8):
                        ins = nc.tensor.matmul(banks_t[bk][:, :], lhsT=view[:, k, j * 128:(j + 1) * 128],
                                               rhs=oaT[:, k, tb * 512:(tb + 1) * 512], start=(k == 0), stop=(k == 7))
                    pe_end(ins, [slab_b, oa_b[tb]], [bankB[bk]])
                    dst = SGA[:, j, tb * 512:(tb + 1) * 512]
                    op(DVE, [bankB[bk]], [sga_b[(j, tb)]], lambda: nc.vector.tensor_tensor(
                        out=dst, in0=banks_t[bk][:, :], in1=dst, op=ALU.mult))
            ring_release()
            slab_b, view = ring_next(("win", 7168 + cq * 512))
            for j in range(4):
                for tb in range(2):
                    bk = nb(allb)
                    fm_group(bk, slab_b, view, j, hT, hT_b[tb], tb * 512, 512)
                    op(ACT, [bankB[bk]], [sgb_b[(j, tb)]], lambda: nc.scalar.activation(
                        out=SGB[:, j, tb * 512:(tb + 1) * 512], in_=banks_t[bk][:, :], func=AF.Sigmoid))
            ring_release()
            for (m, s) in a4_plan[(cq, 1)]:
                do_ada_slab(m, s)
            if cq == 0:
                finish_mod(2)
                finish_mod(3)
            if cq == 2:
                finish_mod(4)
            slab_b, view = ring_next(("wpb", cq))
            for j in range(4):
                for tb in range(2):
                    bk = nb(allb)
                    pe_begin([slab_b] + ob_tb[tb], [bankB[bk]])
                    for k in range(8):
                        ins = nc.tensor.matmul(banks_t[bk][:, :], lhsT=view[:, k, j * 128:(j + 1) * 128],
                                               rhs=obT[:, k, tb * 512:(tb + 1) * 512], start=(k == 0), stop=(k == 7))
                    pe_end(ins, [slab_b] + ob_tb[tb], [bankB[bk]])
                    i = tmpc[0] % 3
                    tmpc[0] += 1
                    op(DVE, [bankB[bk], sgb_b[(j, tb)]], [tmp_b[i]], lambda: nc.vector.tensor_tensor(
                        out=tmp_v[i], in0=banks_t[bk][:, :], in1=SGB[:, j, tb * 512:(tb + 1) * 512], op=ALU.mult))
                    op(DVE, [tmp_b[i], sga_b[(j, tb)]], [m_b[tb]], lambda: nc.vector.tensor_tensor(
                        out=mT[:, cq * 4 + j, tb * 512:(tb + 1) * 512], in0=tmp_v[i],
                        in1=SGA[:, j, tb * 512:(tb + 1) * 512], op=ALU.add))
            ring_release()
        rtile.free()
        sgb_t.free()
        sga_t.free()
        gu_t.free()
        tr.retire(oa_b)
        qT_t.free()
        hT_t.free()

        def build_bcast(Gmod, v, dst, dst_b):
            for cb in range(4):
                i = tmpc[0] % 3
                tmpc[0] += 1
                dg = tmp_v[i].rearrange("p (c q) -> p c q", q=128)
                g4 = Gmod[:, cb * 4:cb * 4 + 4, v:v + 1].to_broadcast([128, 4, 128])
                o4 = onesf.unsqueeze(1).to_broadcast([128, 4, 128])
                op(DVE, [cst_b, A_b], [tmp_b[i]], lambda: nc.vector.tensor_tensor(out=dg, in0=o4, in1=g4, op=ALU.mult))
                bk = nb(allb)
                pe_begin([tmp_b[i], cst_b], [bankB[bk]])
                for cc in range(4):
                    ins = nc.tensor.transpose(out=banks_t[bk][:, cc * 128:(cc + 1) * 128], in_=dg[:, cc, :], identity=identf)
                pe_end(ins, [tmp_b[i], cst_b], [bankB[bk]])
                op(ACT, [bankB[bk]], [dst_b], lambda: nc.scalar.copy(out=dst[:, cb * 512:(cb + 1) * 512], in_=banks_t[bk][:, :]))

        h2T_t = Tile("h2T", 16 * 1024 * 2)
        h2T = h2T_t.bf().rearrange("p (k t) -> p k t", t=1024)
        h2_b = [[h2T_t.buf("_Pe"), h2T_t.buf("_Po")], [h2T_t.buf("_Se"), h2T_t.buf("_So")]]
        Y_t = Tile("Y", 4 * D * 4)
        Y = Y_t.f32().rearrange("p (t d) -> p t d", d=D)
        Y_b = [Y_t.buf(f"_{t}") for t in range(4)]
        Yg_b = Y_t.buf("_g")
        gn_t = Tile("gn", D * 4)
        GN = gn_t.f32()
        gn_b = gn_t.buf()
        xt_t = Tile("xt", 2 * D * 4)
        xt_b = [xt_t.buf("_0"), xt_t.buf("_1")]
        xt_v = [xt_t.f32(D, i * D * 4) for i in range(2)]
        x1_d = [Buf(f"x1d{t}") for t in range(8)]
        B2 = MODS[:, 3]
        for half in range(2):
            build_bcast(G1, half, GN, gn_b)
            ssy = stat_cols(16).rearrange("p (t c) -> p t c", c=4)
            ssb = tr.buf("ssy")

            xsrc = []
            for t in range(4):
                gt = half * 4 + t
                if t < 2:
                    xv, xb = xt_v[t].rearrange("p (c f) -> p c f", f=256), xt_b[t]
                else:
                    c0 = (t - 2) * 8
                    xv, xb = h2T[:, c0:c0 + 8, half * 512:(half + 1) * 512].bitcast(F32), h2_b[half][t - 2]
                dma(SP, xv, xin[gt * 128:(gt + 1) * 128, :].rearrange("p (c f) -> p c f", f=256), [], h2_b[half] if t >= 2 else [xb], xb)
                xsrc.append((xv, xb))
            for cb in range(4):
                slab_b, view = ring_next(("wo", half, cb))
                for t in range(4):
                    bk = nb(allb)
                    tm_group(bk, slab_b, view, mT, [m_b[half]], half * 512 + t * 128)
                    jv, jb = jq()
                    op(ACT, [bankB[bk]], [ssb, jb], lambda: nc.scalar.activation(
                        out=jv, in_=banks_t[bk][:, :], func=AF.Square, accum_out=ssy[:, t, cb:cb + 1]))
                    op(DVE, [bankB[bk], gn_b], [Y_b[t]], lambda: nc.vector.tensor_tensor(
                        out=Y[:, t, cb * 512:(cb + 1) * 512], in0=banks_t[bk][:, :], in1=GN[:, cb * 512:(cb + 1) * 512], op=ALU.mult))
                ring_release()
            rsy = stat_cols(4)
            op(DVE, [ssb], [ssb], lambda: nc.vector.tensor_reduce(out=rsy, in_=ssy, axis=mybir.AxisListType.X, op=ALU.add))
            op(DVE, [ssb], [ssb], lambda: nc.vector.tensor_scalar(
                out=rsy, in0=rsy, scalar1=1.0 / D, scalar2=EPS, op0=ALU.mult, op1=ALU.add))
            op(ACT, [ssb], [ssb], lambda: nc.scalar.activation(out=rsy, in_=rsy, func=AF.Sqrt))
            op(DVE, [ssb], [ssb], lambda: nc.vector.reciprocal(out=rsy, in_=rsy))
            ss2h = stat_cols(4)
            rs2h = stat_cols(4)
            s2b = tr.buf("ss2h")
            for t in range(4):
                gt = half * 4 + t
                xv, xb = xsrc[t]
                yv = Y[:, t, :].rearrange("p (c f) -> p c f", f=256)
                op(DVE, [ssb] + (h2_b[half] if t >= 2 else [xb]), [Y_b[t]], lambda: nc.vector.scalar_tensor_tensor(
                    out=yv, in0=yv, scalar=rsy[:, t:t + 1], in1=xv, op0=ALU.mult, op1=ALU.add))
                dma(ACT, x1s[gt * 128:(gt + 1) * 128, :], Y[:, t, :], [Y_b[t]], [x1_d[gt]], Y_b[t])
                op(ACT, [Y_b[t]], [s2b] + junk_b, lambda: nc.scalar.activation(
                    out=junk, in_=Y[:, t, :], func=AF.Square, accum_out=ss2h[:, t:t + 1]))
            op(DVE, [s2b], [s2b], lambda: nc.vector.tensor_scalar(
                out=rs2h, in0=ss2h, scalar1=1.0 / D, scalar2=EPS, op0=ALU.mult, op1=ALU.add))
            op(ACT, [s2b], [s2b], lambda: nc.scalar.activation(out=rs2h, in_=rs2h, func=AF.Sqrt))
            op(DVE, [s2b], [s2b], lambda: nc.vector.reciprocal(out=rs2h, in_=rs2h))
            dg = tmp_t.f32(512).rearrange("p (t q) -> p t q", q=128)
            for t in range(4):
                op(DVE, [s2b, cst_b], [tmp_b[0]], lambda: nc.vector.tensor_scalar(
                    out=dg[:, t, :], in0=identf, scalar1=rs2h[:, t:t + 1], scalar2=None, op0=ALU.mult))
            Yg_b.w, Yg_b.r = {}, {}
            for t in range(4):
                _merge(Yg_b.w, Y_b[t].w)
                _merge(Yg_b.r, Y_b[t].r)
            transpose_mod(Y, Yg_b, 4, A2, B2, half, h2T, h2_b[half], half * 512, allb, diag=dg, diag_b=tmp_b[0])
            for t in range(4):
                Y_b[t].w = dict(Yg_b.w)
                Y_b[t].r = dict(Yg_b.r)
        xt_t.free()
        gn_t.free()
        Y_t.free()
        mT_t.free()

        act_t = Tile("actT", 44 * 1024 * 2)
        actT = act_t.bf().rearrange("p (c t) -> p c t", t=1024)
        act_b = [act_t.buf("_P"), act_t.buf("_S")]
        sg_b, sg_v, sgc = tmp_b, tmp_v, tmpc
        for cbp in range(22):
            gu_sb, (gview, uview) = ring_next(("wgu", cbp))
            for j in range(2):
                for tb in range(2):
                    bg = nb(allb)
                    fm_group(bg, gu_sb, gview, j, h2T, h2_b[tb], tb * 512, 512)
                    bu = nb(allb)
                    fm_group(bu, gu_sb, uview, j, h2T, h2_b[tb], tb * 512, 512)
                    i = sgc[0] % 3
                    sgc[0] += 1
                    op(ACT, [bankB[bg]], [sg_b[i]], lambda: nc.scalar.activation(out=sg_v[i], in_=banks_t[bg][:, :], func=AF.Silu))
                    op(DVE, [bankB[bu], sg_b[i]], [act_b[tb]], lambda: nc.vector.tensor_tensor(
                        out=actT[:, cbp * 2 + j, tb * 512:(tb + 1) * 512], in0=banks_t[bu][:, :], in1=sg_v[i], op=ALU.mult))
            ring_release()
        h2T_t.free()

        gn2_t = Tile("gn2", 2 * D * 4)
        gn2_b = [gn2_t.buf("_0"), gn2_t.buf("_1")]
        gn2_v = [gn2_t.f32(D, i * D * 4) for i in range(2)]
        build_bcast(G2, 0, gn2_v[0], gn2_b[0])
        build_bcast(G2, 1, gn2_v[1], gn2_b[1])
        ys_t = Tile("ystg", 4 * 512 * 4)
        ys_b = [ys_t.buf(f"_{i}") for i in range(4)]
        ys_v = [ys_t.f32(512, i * 2048) for i in range(4)]
        ysc = [0]
        ss2 = stat_cols(32).rearrange("p (t c) -> p t c", c=4)
        rs2 = stat_cols(8)
        ss2b = [tr.buf(f"ss2_{t}") for t in range(8)]
        y2_d = [Buf(f"y2d{t}") for t in range(8)]
        fxs = [None] * 8
        z012 = [None] * 8
        z3 = [None] * 8
        act_inh = {}

        def carve_slot(n_slab, tiles):
            i = n_slab % NSLOT
            tr.retire([slot_bufs[i]])
            base = ring_t.off + i * SLOT_ELEMS * 2
            for j, t in enumerate(tiles):
                zb = tr.buf(f"z012_{t}")
                zv = sb.f32(base + j * 8192, 1536)
                dma(SP, zv, y2s[t * 128:(t + 1) * 128, 0:1536], [y2_d[t]], [zb], zb)
                z012[t] = (zv, zb)
                z3[t] = (sb.f32(base + j * 8192 + 6144, 512), tr.buf(f"z3_{t}"))

        def prefetch_x1(tiles):
            inh = {}
            for ab in act_b:
                _merge(inh, ab.w)
                _merge(inh, ab.r)
            for t in tiles:
                fb = Buf(f"fx{t}", inherit=inh)
                fv = sb.f32(act_t.off + t * D * 4, D)
                dma(ACT, fv, x1s[t * 128:(t + 1) * 128, :], [x1_d[t]], [fb], fb)
                fxs[t] = (fv, fb)

        def final_rs_a(t):
            sbt = ss2b[t]
            r = rs2[:, t:t + 1]
            op(DVE, [sbt], [sbt], lambda: nc.vector.tensor_reduce(out=r, in_=ss2[:, t, :], axis=mybir.AxisListType.X, op=ALU.add))
            op(DVE, [sbt], [sbt], lambda: nc.vector.tensor_scalar(out=r, in0=r, scalar1=1.0 / D, scalar2=EPS, op0=ALU.mult, op1=ALU.add))
            op(ACT, [sbt], [sbt], lambda: nc.scalar.activation(out=r, in_=r, func=AF.Sqrt))

        def final_tile(t):
            fx, fxb = fxs[t]
            zv, zb = z012[t]
            z3v, z3b = z3[t]
            sbt = ss2b[t]
            r = rs2[:, t:t + 1]
            op(DVE, [sbt], [sbt], lambda: nc.vector.reciprocal(out=r, in_=r))
            zbs = list(zb) if isinstance(zb, list) else [zb]
            op(DVE, [sbt] + zbs, [fxb], lambda: nc.vector.scalar_tensor_tensor(
                out=fx[:, 0:1536], in0=zv, scalar=r, in1=fx[:, 0:1536], op0=ALU.mult, op1=ALU.add))
            op(DVE, [sbt, z3b], [fxb], lambda: nc.vector.scalar_tensor_tensor(
                out=fx[:, 1536:2048], in0=z3v, scalar=r, in1=fx[:, 1536:2048], op0=ALU.mult, op1=ALU.add))
            dst = yp_d[t * 128:(t + 1) * 128, :] if t < 4 else ys_d[(t - 4) * 128:(t - 3) * 128, :]
            dma(POOL, dst, fx, [fxb], [], fxb, is_output=True)

        for cb in range(4):
            for kk, (r0, kc) in enumerate(KK):
                slab_b, view = ring_next(("wd", cb, kk))
                n_this = ring["consumed"] - 1
                last = (cb == 3 and kk == 2)
                for t in range(8):
                    rd = [slab_b, act_b[t // 4]]
                    pe_begin(rd, [bankB[t]])
                    for k in range(kc):
                        ins = nc.tensor.matmul(banks_t[t][:, :], lhsT=actT[:, r0 + k, t * 128:(t + 1) * 128],
                                               rhs=view[:, k, :], start=(kk == 0 and k == 0), stop=(kk == 2 and k == kc - 1))
                    pe_end(ins, rd, [bankB[t]])
                    if last and t == 7:
                        inh = {}
                        for ab in act_b:
                            _merge(inh, ab.w)
                            _merge(inh, ab.r)
                        zb7 = Buf("z012_7", inherit=inh)
                        zv7 = sb.f32(act_t.off + 32 * 2048, 1536)
                        dma(SP, zv7, y2s[7 * 128:8 * 128, 0:1536], [y2_d[7]], [zb7], zb7)
                        z012[7] = (zv7, zb7)
                    if kk == 2:
                        if last and t >= 1:
                            final_rs_a(t - 1)
                        jv, jb = jq()
                        op(ACT, [bankB[t]], [ss2b[t], jb], lambda: nc.scalar.activation(
                            out=jv, in_=banks_t[t][:, :], func=AF.Square, accum_out=ss2[:, t, cb:cb + 1]))
                        if not last:
                            i = ysc[0] % 4
                            ysc[0] += 1
                            op(DVE, [bankB[t], gn2_b[t // 4]], [ys_b[i]], lambda: nc.vector.tensor_tensor(
                                out=ys_v[i], in0=banks_t[t][:, :], in1=gn2_v[t // 4][:, cb * 512:(cb + 1) * 512], op=ALU.mult))
                            dma(SP, y2s[t * 128:(t + 1) * 128, cb * 512:(cb + 1) * 512], ys_v[i], [ys_b[i]], [y2_d[t]], ys_b[i])
                        else:
                            if t >= 4:
                                z3[t] = (ys_v[t - 4], ys_b[t - 4])
                            z3v, z3b = z3[t]
                            op(DVE, [bankB[t], gn2_b[t // 4]], [z3b], lambda: nc.vector.tensor_tensor(
                                out=z3v, in0=banks_t[t][:, :], in1=gn2_v[t // 4][:, cb * 512:(cb + 1) * 512], op=ALU.mult))
                            if t >= 1:
                                final_tile(t - 1)
                ring_release()
                if cb == 3 and kk == 0:
                    prefetch_x1([0, 1, 2, 3])
                    carve_slot(n_this, [0, 1])
                if cb == 3 and kk == 1:
                    carve_slot(n_this, [2, 3])
                    prefetch_x1([4, 5, 6, 7])
                    for t, v in ((4, 0), (5, 1)):
                        inh = {}
                        _merge(inh, gn2_b[v].w)
                        _merge(inh, gn2_b[v].r)
                        zb = Buf(f"z012_{t}", inherit=inh)
                        zv = gn2_v[v][:, 0:1536]
                        dma(SP, zv, y2s[t * 128:(t + 1) * 128, 0:1536], [y2_d[t]], [zb], zb)
                        z012[t] = (zv, zb)
                    zv = tmp_t.f32(1536)
                    dma(SP, zv, y2s[6 * 128:7 * 128, 0:1536], [y2_d[6]], tmp_b, tmp_b[0])
                    z012[6] = (zv, tmp_b)
        final_rs_a(7)
        final_tile(7)

        for k, (s, v) in final_events.items():
            SP.wait_ev(s, v)
        assert ring["consumed"] == len(ring["sched"]) == ring["released"], (ring["consumed"], len(ring["sched"]), ring["released"])
    return nc


_PROG = {}


def _bias_table(rpb, half):
    H = rpb.shape[0]
    lr = np.arange(12)
    gr = np.where(lr < 8, lr + 8 * half, (lr if half == 0 else lr - 4))
    qi = np.arange(8)
    r = qi + 8 * half
    rs = np.clip(r - 4, 0, 8)
    row_ok = (gr[None, :] >= rs[:, None]) & (gr[None, :] < rs[:, None] + 8)
    if half == 0:
        row_ok[:, 11] = row_ok[:, 11]
    dr = gr[None, :] - r[:, None] + 7
    cols = np.arange(64)
    cs = np.clip(cols - 8, 0, 48)
    col_ok = (cols[None, :] >= cs[:, None]) & (cols[None, :] < cs[:, None] + 16)
    dc = np.clip(cols[None, :] - cols[:, None] + 15, 0, 30)
    drc = np.clip(dr, 0, 14)
    g = rpb[:, drc[:, :, None, None], dc[None, None, :, :]]
    ok = row_ok[:, :, None, None] & col_ok[None, None, :, :]
    g = np.where(ok[None], g, np.float32(NEG)).astype(np.float32)
    g = g.transpose(0, 2, 4, 1, 3).reshape(H, 12 * 64, 512)
    g = g.reshape(H, 6, 128, 512).transpose(0, 2, 1, 3)
    return np.ascontiguousarray(g)


def kernel(x_prompt, x_sample, cache_k, cache_v, c, c_ctx, w_ada, b_ada,
           norm_mix_pre, norm_mix_post, norm_ffn_pre, norm_ffn_post, w_in, rpb,
           ln_v, w_s, b_s, w_pa, w_pb, w_o, w_gate, w_up, w_down):
    f = lambda a: np.ascontiguousarray(np.asarray(a, dtype=np.float32))
    x_prompt, x_sample, cache_k, cache_v = f(x_prompt), f(x_sample), f(cache_k), f(cache_v)
    c, c_ctx = f(c), f(c_ctx)
    if "nc" not in _PROG:
        _PROG["nc"] = build_program()
    nc = _PROG["nc"]
    shared = {
        "w_ada": f(w_ada)[0], "w_in": f(w_in)[0], "w_pa": f(w_pa)[0], "w_pb": f(w_pb)[0], "w_o": f(w_o)[0],
        "w_gate": f(w_gate)[0], "w_up": f(w_up)[0], "w_down": f(w_down)[0],
        "b_adaT": np.ascontiguousarray(f(b_ada)[0].reshape(6, 16, 128).transpose(2, 0, 1)),
        "nvT": np.ascontiguousarray(np.stack([f(norm_mix_pre)[0], f(norm_mix_post)[0], f(norm_ffn_pre)[0],
                                              f(norm_ffn_post)[0]]).reshape(4, 16, 128).transpose(2, 0, 1)),
        "ln_v": f(ln_v)[0].reshape(1, 1024),
        "w_sT": np.ascontiguousarray(f(w_s)[0].transpose(2, 0, 1)),
        "b_s": f(b_s)[0].reshape(1, 1024),
    }
    bias_tabs = [_bias_table(f(rpb)[0], 0), _bias_table(f(rpb)[0], 1)]
    in_maps = []
    for i in range(8):
        b, half = i // 2, i % 2
        xs = x_sample[b]
        own = xs[half * 512:(half + 1) * 512]
        halo = xs[512:768] if half == 0 else xs[256:512]
        xin = np.concatenate([x_prompt[2 * i:2 * i + 2].reshape(512, D), own, halo], axis=0)
        cv2 = np.stack([c_ctx, c[b]], axis=-1)
        m = dict(shared)
        m.update({
            "xin": np.ascontiguousarray(xin),
            "ck": np.ascontiguousarray(cache_k[b, 0].reshape(512, 1024)),
            "cv": np.ascontiguousarray(cache_v[b, 0].reshape(512, 1024)),
            "cvT": np.ascontiguousarray(cv2.reshape(16, 128, 2).transpose(1, 0, 2)),
            "biasT": bias_tabs[half],
        })
        in_maps.append(m)
    res = run_bass_kernel_spmd(nc, in_maps, core_ids=list(range(8)))
    R = res.results
    y_prompt = np.stack([R[i]["yp"].reshape(2, 256, D) for i in range(8)]).reshape(16, 256, D)
    y_sample = np.stack([R[i]["ys"] for i in range(8)]).reshape(4, 1024, D)
    state_k = np.stack([R[i]["sk"].reshape(2, 1, 256, NH, DH) for i in range(8)]).reshape(16, 1, 256, NH, DH)
    state_v = np.stack([R[i]["sv"].reshape(2, 1, 256, NH, DH) for i in range(8)]).reshape(16, 1, 256, NH, DH)
    return (y_prompt.astype(np.float32), y_sample.astype(np.float32),
            state_k.astype(np.float32), state_v.astype(np.float32))
```
